# Optimizing a Trainium2 kernel written in Bass

```python
import jax, jax.numpy as jnp
from jax import lax
import numpy as np

D_MODEL = 1024
BATCH = 4
SEQ = 8192
DEPTH = 1
DEC_BATCH = 4
DEC_SEQ = 4096
PAST_LEN = 128

GRID_W = 64
N_HEADS = 8
N_KV_HEADS = 2
HEAD_DIM = 64
ATTN_WIDTH = N_HEADS * HEAD_DIM
KV_WIDTH = N_KV_HEADS * HEAD_DIM
POOL_WINDOWS = (2, 4, 8, 16)
POOL_GROUPS = len(POOL_WINDOWS)
POOL_WIDTH = D_MODEL // 2
POOL_GROUP_WIDTH = POOL_WIDTH // POOL_GROUPS
D_FF = 2816
PLE_DIM = 256
Q_BLOCK = 128
ROPE_THETA = 10000.0
EPS = 1e-6
SPLITS = (ATTN_WIDTH,
          ATTN_WIDTH + KV_WIDTH,
          ATTN_WIDTH + 2 * KV_WIDTH,
          ATTN_WIDTH + 2 * KV_WIDTH + POOL_WIDTH,
          ATTN_WIDTH + 2 * KV_WIDTH + POOL_WIDTH + D_MODEL)
IN_WIDTH = ATTN_WIDTH + 2 * KV_WIDTH + POOL_WIDTH + 2 * D_MODEL

kernel_name = 'hybrid_gqa_axial_rope_multiscale_pool_macaron_encoder'


def rmsnorm(x, g):
    xf = x.astype(jnp.float32)
    y = xf * lax.rsqrt(jnp.mean(xf * xf, axis=-1, keepdims=True) + EPS)
    return (y * g.astype(jnp.float32)).astype(x.dtype)


def swiglu(x, w_gate, w_up, w_down):
    return (jax.nn.silu(x @ w_gate) * (x @ w_up)) @ w_down


def axial_rope_tables(T):
    rows = T // GRID_W
    row = jnp.repeat(jnp.arange(rows, dtype=jnp.float32), GRID_W)
    col = jnp.broadcast_to(jnp.arange(GRID_W, dtype=jnp.float32)[None, :], (rows, GRID_W)).reshape(-1)
    half = HEAD_DIM // 2
    inv = ROPE_THETA ** (-jnp.arange(0, half, 2, dtype=jnp.float32) / half)
    ang = jnp.stack([row[:, None] * inv, col[:, None] * inv], axis=1)
    return jnp.cos(ang), jnp.sin(ang)


def apply_rope(x, cos, sin):
    B, T, H, _ = x.shape
    xr = x.reshape(B, T, H, 2, 2, HEAD_DIM // 4)
    x1 = xr[..., 0, :]
    x2 = xr[..., 1, :]
    c = cos[None, :, None].astype(x.dtype)
    s = sin[None, :, None].astype(x.dtype)
    out = jnp.stack([x1 * c - x2 * s, x2 * c + x1 * s], axis=-2)
    return out.reshape(B, T, H, HEAD_DIM)


def blocked_gqa(q, k, v):
    B, T = q.shape[:2]
    nblk = T // Q_BLOCK
    G = N_HEADS // N_KV_HEADS
    qb = q.reshape(B, nblk, Q_BLOCK, N_KV_HEADS, G, HEAD_DIM).swapaxes(0, 1)
    scale = HEAD_DIM ** -0.5

    def block(qi):
        s = jnp.einsum('bqkgd,bskd->bkgqs', qi, k).astype(jnp.float32) * scale
        p = jax.nn.softmax(s, axis=-1).astype(v.dtype)
        return jnp.einsum('bkgqs,bskd->bqkgd', p, v)

    o = lax.map(block, qb)
    return o.swapaxes(0, 1).reshape(B, T, ATTN_WIDTH)


def centred_mean_minus_self(z, w):
    T = z.shape[1]
    left = w // 2
    right = w - 1 - left
    zf = z.astype(jnp.float32)
    csum = jnp.concatenate([jnp.zeros_like(zf[:, :1]), jnp.cumsum(zf, axis=1)], axis=1)
    t = jnp.arange(T)
    lo = jnp.maximum(t - left, 0)
    hi = jnp.minimum(t + right, T - 1)
    total = jnp.take(csum, hi + 1, axis=1) - jnp.take(csum, lo, axis=1)
    cnt = (hi - lo + 1).astype(jnp.float32)[None, :, None]
    return (total / cnt - zf).astype(z.dtype)


def multiscale_pool(z, pool_w, pool_scale):
    B, T = z.shape[:2]
    zg = z.reshape(B, T, POOL_GROUPS, POOL_GROUP_WIDTH)
    pooled = jnp.stack([centred_mean_minus_self(zg[:, :, g], POOL_WINDOWS[g]) for g in range(POOL_GROUPS)], axis=2)
    mixed = jnp.einsum('btgc,gcd->btgd', pooled, pool_w).reshape(B, T, POOL_WIDTH)
    return mixed * pool_scale


def encoder_layer(x, p, ffn1_norm, ffn1_w_gate, ffn1_w_up, ffn1_w_down, mix_norm, w_in, q_norm, k_norm,
                  w_up_attn, pool_w, pool_scale, w_up_pool, w_out, ffn2_norm, ffn2_w_gate, ffn2_w_up,
                  ffn2_w_down, ple_gate_norm, w_ple_gate, w_ple_proj, ple_post_norm):
    B, T = x.shape[:2]
    h = x + 0.5 * swiglu(rmsnorm(x, ffn1_norm), ffn1_w_gate, ffn1_w_up, ffn1_w_down)
    u = rmsnorm(h, mix_norm)
    proj = u @ w_in
    q, k, v, z, ga, gb = jnp.split(proj, SPLITS, axis=-1)
    q = rmsnorm(q.reshape(B, T, N_HEADS, HEAD_DIM), q_norm)
    k = rmsnorm(k.reshape(B, T, N_KV_HEADS, HEAD_DIM), k_norm)
    v = v.reshape(B, T, N_KV_HEADS, HEAD_DIM)
    cos, sin = axial_rope_tables(T)
    q = apply_rope(q, cos, sin)
    k = apply_rope(k, cos, sin)
    a = blocked_gqa(q, k, v) @ w_up_attn
    b = multiscale_pool(z, pool_w, pool_scale) @ w_up_pool
    m = jax.nn.sigmoid(ga) * a + jax.nn.sigmoid(gb) * b
    h = h + m @ w_out
    h = h + 0.5 * swiglu(rmsnorm(h, ffn2_norm), ffn2_w_gate, ffn2_w_up, ffn2_w_down)
    gate = jax.nn.sigmoid(rmsnorm(h, ple_gate_norm) @ w_ple_gate)
    e = rmsnorm(p @ w_ple_proj, ple_post_norm)
    return h + gate * e


def setup_inputs(seed: int = 0) -> dict:
    key = jax.random.key(seed)
    ks = jax.random.split(key, 32)
    f32 = jnp.float32

    def nrm(k, shape, fan_in):
        return jax.random.normal(k, shape, f32) * (fan_in ** -0.5)

    def gain(k, shape):
        return 1.0 + 0.05 * jax.random.normal(k, shape, f32)

    L = DEPTH
    return {
        'x_prompt': jax.random.normal(ks[0], (BATCH, SEQ, D_MODEL), f32),
        'x_sample': jax.random.normal(ks[1], (DEC_BATCH, DEC_SEQ, D_MODEL), f32),
        'p_prompt': jax.random.normal(ks[2], (DEPTH, BATCH, SEQ, PLE_DIM), f32),
        'p_sample': jax.random.normal(ks[3], (DEPTH, DEC_BATCH, DEC_SEQ, PLE_DIM), f32),
        'ffn1_norm': gain(ks[4], (L, D_MODEL)),
        'ffn1_w_gate': nrm(ks[5], (L, D_MODEL, D_FF), D_MODEL),
        'ffn1_w_up': nrm(ks[6], (L, D_MODEL, D_FF), D_MODEL),
        'ffn1_w_down': nrm(ks[7], (L, D_FF, D_MODEL), D_FF),
        'mix_norm': gain(ks[8], (L, D_MODEL)),
        'w_in': nrm(ks[9], (L, D_MODEL, IN_WIDTH), D_MODEL),
        'q_norm': gain(ks[10], (L, HEAD_DIM)),
        'k_norm': gain(ks[11], (L, HEAD_DIM)),
        'w_up_attn': nrm(ks[12], (L, ATTN_WIDTH, D_MODEL), ATTN_WIDTH),
        'pool_w': nrm(ks[13], (L, POOL_GROUPS, POOL_GROUP_WIDTH, POOL_GROUP_WIDTH), POOL_GROUP_WIDTH),
        'pool_scale': 1.0 + 0.1 * jax.random.normal(ks[14], (L, POOL_WIDTH), f32),
        'w_up_pool': nrm(ks[15], (L, POOL_WIDTH, D_MODEL), POOL_WIDTH),
        'w_out': nrm(ks[16], (L, D_MODEL, D_MODEL), D_MODEL),
        'ffn2_norm': gain(ks[17], (L, D_MODEL)),
        'ffn2_w_gate': nrm(ks[18], (L, D_MODEL, D_FF), D_MODEL),
        'ffn2_w_up': nrm(ks[19], (L, D_MODEL, D_FF), D_MODEL),
        'ffn2_w_down': nrm(ks[20], (L, D_FF, D_MODEL), D_FF),
        'ple_gate_norm': gain(ks[21], (L, D_MODEL)),
        'w_ple_gate': nrm(ks[22], (L, D_MODEL, D_MODEL), D_MODEL),
        'w_ple_proj': nrm(ks[23], (L, PLE_DIM, D_MODEL), PLE_DIM),
        'ple_post_norm': gain(ks[24], (L, D_MODEL)),
    }


def reference(x_prompt, x_sample, p_prompt, p_sample, ffn1_norm, ffn1_w_gate, ffn1_w_up, ffn1_w_down,
              mix_norm, w_in, q_norm, k_norm, w_up_attn, pool_w, pool_scale, w_up_pool, w_out,
              ffn2_norm, ffn2_w_gate, ffn2_w_up, ffn2_w_down, ple_gate_norm, w_ple_gate, w_ple_proj,
              ple_post_norm):
    weights = (ffn1_norm, ffn1_w_gate, ffn1_w_up, ffn1_w_down, mix_norm, w_in, q_norm, k_norm,
               w_up_attn, pool_w, pool_scale, w_up_pool, w_out, ffn2_norm, ffn2_w_gate, ffn2_w_up,
               ffn2_w_down, ple_gate_norm, w_ple_gate, w_ple_proj, ple_post_norm)

    def trunk(x, p):
        h = x
        for i in range(DEPTH):
            h = encoder_layer(h, p[i], *[w[i] for w in weights])
        return h

    y_prompt = trunk(x_prompt, p_prompt)
    y_sample = trunk(x_sample, p_sample)
    return (y_prompt, y_sample)
```

```python
import numpy as np
from contextlib import ExitStack
import concourse.bass as bass
import concourse.mybir as mybir
from concourse.bass_utils import run_bass_kernel_spmd

F32 = mybir.dt.float32
BF16 = mybir.dt.bfloat16
ALU = mybir.AluOpType
AF = mybir.ActivationFunctionType

D = 1024
DFF = 2816
NJ = 22
TP, TS = 8192, 4096
NT = 512
NLOC = TP + TS
NOWN = NLOC // 2
EPS = 1e-6
SLOT = 4096
import os
STAGE = int(os.environ.get("KDEBUG_STAGE", "0"))


class Buf:
    __slots__ = ("name", "writers", "readers", "sem", "ndma")

    def __init__(self, name):
        self.name = name
        self.writers = []
        self.readers = []
        self.sem = None
        self.ndma = 0


class Op:
    __slots__ = ("eng", "fn", "deps", "signal", "val", "sem", "is_dma", "dmabuf")

    def __init__(self, eng, fn, is_dma=False, dmabuf=None):
        self.eng = eng
        self.fn = fn
        self.deps = []
        self.signal = False
        self.val = None
        self.sem = None
        self.is_dma = is_dma
        self.dmabuf = dmabuf


ENGS = ("pe", "act", "dve", "pool", "sp")


def _push(lst, op):
    if not op.is_dma:
        for i, o in enumerate(lst):
            if (not o.is_dma) and o.eng == op.eng:
                lst[i] = op
                return
    lst.append(op)


class Sched:
    def __init__(self, nc, stack):
        self.nc = nc
        self.stack = stack
        self.ops = {e: [] for e in ENGS}
        self.out_dma_ops = []

    def buf(self, name):
        return Buf(name)

    def bufs(self, name, n):
        return [Buf(f"{name}{i}") for i in range(n)]

    def _add(self, op, reads, writes, pwrites):
        deps = {}
        for b in reads:
            for w in b.writers:
                deps[id(w)] = w
        for b in writes:
            for w in b.writers:
                deps[id(w)] = w
            for r in b.readers:
                deps[id(r)] = r
        for b in pwrites:
            for r in b.readers:
                deps[id(r)] = r
            for w in b.writers:
                if (not w.is_dma) and (op.is_dma or w.eng != op.eng):
                    deps[id(w)] = w
        for d in deps.values():
            if d is op:
                continue
            if (not d.is_dma) and (not op.is_dma) and d.eng == op.eng and op.eng == "pe":
                continue
            op.deps.append(d)
        for b in reads:
            _push(b.readers, op)
        for b in writes:
            b.writers = [op]
            b.readers = []
        for b in pwrites:
            _push(b.writers, op)
            b.readers = []
        self.ops[op.eng].append(op)
        return op

    def op(self, eng, fn, reads=(), writes=(), pwrites=()):
        return self._add(Op(eng, fn), list(reads), list(writes), list(pwrites))

    def dma(self, eng, fn, sembuf, reads=(), writes=(), pwrites=(), is_out=False):
        op = Op(eng, fn, is_dma=True, dmabuf=sembuf)
        self._add(op, list(reads), list(writes), list(pwrites))
        if is_out:
            self.out_dma_ops.append(op)
        return op

    def emit(self):
        nc = self.nc
        for e in ENGS:
            for op in self.ops[e]:
                for d in op.deps:
                    d.signal = True
        esem = {e: self.stack.enter_context(nc.semaphore(f"s_{e}")) for e in ENGS}
        for e in ENGS:
            cnt = 0
            for op in self.ops[e]:
                if op.is_dma:
                    b = op.dmabuf
                    if b.sem is None:
                        b.sem = self.stack.enter_context(nc.semaphore(f"d_{b.name}"))
                    b.ndma += 1
                    op.sem = b.sem
                    op.val = 16 * b.ndma
                elif op.signal:
                    cnt += 1
                    op.sem = esem[e]
                    op.val = cnt
        block = self.stack.enter_context(nc.Block())

        def run(e, eng):
            seen = {}
            for op in self.ops[e]:
                need = {}
                for d in op.deps:
                    k = id(d.sem)
                    if seen.get(k, 0) >= d.val:
                        continue
                    if k not in need or need[k][1] < d.val:
                        need[k] = (d.sem, d.val)
                for k, (s, v) in need.items():
                    eng.wait_ge(s, v)
                    seen[k] = v
                ins = op.fn(eng)
                if op.is_dma:
                    ins.then_inc(op.sem, 16)
                elif op.signal:
                    ins.then_inc(op.sem, 1)
            if e == "sp":
                for op in self.out_dma_ops:
                    k = id(op.sem)
                    if seen.get(k, 0) < op.val:
                        eng.wait_ge(op.sem, op.val)
                        seen[k] = op.val

        @block.tensor
        def _(eng):
            run("pe", eng)

        @block.scalar
        def _(eng):
            run("act", eng)

        @block.vector
        def _(eng):
            run("dve", eng)

        @block.gpsimd
        def _(eng):
            run("pool", eng)

        @block.sync
        def _(eng):
            run("sp", eng)


def piece_table():
    names = []
    for f in (1, 2):
        for b in range(11):
            names.append((f"gu{f}_{b}", 4096))
        for i in range(8):
            names.append((f"dn{f}_{i}", NJ * 128))
    names += [("kv", 8 * 256), ("q", 4096), ("z", 4096), ("pw", 512), ("upool", 4096),
              ("ga0", 4096), ("ga1", 4096), ("gb0", 4096), ("gb1", 4096), ("uattn", 4096),
              ("wo0", 4096), ("wo1", 4096), ("pg0", 4096), ("pg1", 4096), ("pp", 2048)]
    off = {}
    o = 0
    for n, sz in names:
        off[n] = (o, sz)
        o += sz
    return off, o


PIECES, WS_TOT = piece_table()

GC_F1, GC_MIX, GC_F2, GC_PG, GC_PP, GC_PS, GC_Q, GC_K, GC_FL, GC_EPS = 0, 8, 16, 24, 32, 40, 44, 45, 46, 50
GC_N = 51


def build_nc(wseq_in=None):
    collect = wseq_in is None
    nc = bass.Bass("TRN2", target_bir_lowering=False)
    dt_in = lambda n, s: nc.dram_tensor(n, s, F32, kind="ExternalInput").ap()
    xT = dt_in("xT", [D, NLOC])
    pT = dt_in("pT", [256, NOWN])
    cosT = dt_in("cosT", [128, NLOC])
    sinT = dt_in("sinT", [128, NLOC])
    invcT = dt_in("invcT", [4, NOWN])
    gpack = dt_in("gpack", [128, GC_N])
    cpack = dt_in("cpack", [128, 512])
    w_f = {}
    for f in (1, 2):
        w_f[f] = (dt_in(f"ffn{f}_w_gate", [D, DFF]), dt_in(f"ffn{f}_w_up", [D, DFF]), dt_in(f"ffn{f}_w_down", [DFF, D]))
    w_in = dt_in("w_in", [D, 3328])
    w_up_attn = dt_in("w_up_attn", [512, D])
    pool_w = dt_in("pool_w", [4, 128, 128])
    w_up_pool = dt_in("w_up_pool", [512, D])
    w_out = dt_in("w_out", [D, D])
    w_ple_gate = dt_in("w_ple_gate", [D, D])
    w_ple_proj = dt_in("w_ple_proj", [256, D])
    yT = nc.dram_tensor("yT", [D, NOWN], F32, kind="ExternalOutput").ap()
    dbgT = nc.dram_tensor("dbgT", [128, 7, 512], F32, kind="ExternalOutput").ap() if STAGE == 30 else None
    WS = nc.dram_tensor("WS", [128, WS_TOT], BF16, kind="Internal").ap()
    hS = nc.dram_tensor("hS", [D, NLOC], F32, kind="Internal").ap()

    with ExitStack() as st:
        S = Sched(nc, st)
        sb = lambda n, shp, dt: st.enter_context(nc.sbuf_tensor(n, shp, dt))
        hbuf = [sb(f"hbuf{i}", [128, 8, 528], F32) for i in range(2)]
        ubuf = sb("ubuf", [128, 8, 528], BF16)
        ubuf2 = sb("ubuf2", [128, 8, 528], BF16)
        slb = [sb(f"slb{i}", [128, 512], BF16) for i in range(2)]
        hid = sb("hid", [128, NJ, 512], BF16)
        qT = sb("qT", [128, 4, 512], BF16)
        attnT = sb("attnT", [128, 4, 512], BF16)
        pooled = sb("pooled", [128, 4, 512], BF16)
        zext = sb("zext", [128, 4, 528], F32)
        tmpA = sb("tmpA", [128, 528], F32)
        tmpB = sb("tmpB", [128, 528], F32)
        NPT = 4
        PT = [sb(f"PT{i}", [128, 512], BF16) for i in range(NPT)]
        N32, NBF = 4, 4
        t32 = [sb(f"t32_{i}", [128, 528], F32) for i in range(N32)]
        tbf = [sb(f"tbf_{i}", [128, 528], BF16) for i in range(NBF)]
        cosb = sb("cosb", [128, 512], F32)
        sinb = sb("sinb", [128, 512], F32)
        invcb = sb("invcb", [128, 4, 512], F32)
        NSLOT = 3
        wsl = [sb(f"wsl{i}", [128, SLOT], BF16) for i in range(NSLOT)]
        KT = sb("KT", [128, TP], BF16)
        VA = sb("VA", [128, TP // 128, 256], BF16)
        gp = sb("gp", [128, GC_N], F32)
        cm = sb("cm", [128, 512], BF16)
        pb = sb("pb", [128, 2, 512], BF16)
        rstde = sb("rstde", [128, 512], F32)
        ppb = sb("ppb", [128, 2048], BF16)
        ppstate = {"loaded": False}
        pwv = rstde[:].bitcast(BF16)[:, 0:512]
        ps = [st.enter_context(nc.psum_tensor(f"ps{i}", [128, 512], F32)) for i in range(8)]

        HB = [S.bufs(f"HB{i}_", 8) for i in range(2)]
        UB = S.bufs("UB", 8)
        UB2 = S.bufs("UB2", 8)
        SLB = S.bufs("SLB", 2)
        HID = S.bufs("HID", NJ)
        QT = S.bufs("QT", 4)
        AT = S.bufs("AT", 4)
        PO = S.bufs("PO", 4)
        ZX = S.bufs("ZX", 4)
        TA, TB = S.buf("TA"), S.buf("TB")
        PTB = S.bufs("PTB", NPT)
        T32 = S.bufs("T32_", N32)
        TBF = S.bufs("TBF_", NBF)
        COS, SIN, INVC = S.buf("COS"), S.buf("SIN"), S.buf("INVC")
        WSL = S.bufs("WSL", NSLOT)
        KTB, VAB = S.buf("KTB"), S.buf("VAB")
        GP, CM, PB = S.buf("GP"), S.buf("CM"), S.buf("PB")
        RSTDE = S.buf("RSTDE")
        PPB = S.buf("PPB")
        PS = S.bufs("PSB", 8)
        WSB = {n: S.buf("WS_" + n) for n in PIECES}
        HS = S.bufs("HS", NLOC // NT)
        YS = S.bufs("YS", 2)

        cnt = {"t32": 0, "tbf": 0, "pt": 0, "psr": 0}
        if STAGE == 30:
            dbgb = sb("dbgb", [128, 7, 512], F32)
            DBG = S.buf("DBG")

            def dbg_put(slot, src_ap, src_buf):
                S.op("dve", lambda e: e.tensor_copy(out=dbgb[:, slot, :], in_=src_ap), reads=[src_buf], pwrites=[DBG])

            def dbg_flush():
                S.dma("pool", lambda e: e.dma_start(out=dbgT, in_=dbgb[:]), DBG, reads=[DBG], is_out=True)

        def T32n():
            i = cnt["t32"] % N32
            cnt["t32"] += 1
            return t32[i], T32[i]

        def TBFn():
            i = cnt["tbf"] % NBF
            cnt["tbf"] += 1
            return tbf[i], TBF[i]

        def PTn():
            i = cnt["pt"] % NPT
            cnt["pt"] += 1
            return PT[i], PTB[i]

        def PSRn():
            i = cnt["psr"] % 4
            cnt["psr"] += 1
            return ps[i], PS[i]

        gcol = lambda c: gp[:, c:c + 1]
        ONES = cm[:, 0:128]
        BONES = cm[:, 128:256]
        RMAT = cm[:, 256:384]
        SWAPM = cm[:, 384:512]

        S.dma("sp", lambda e: e.dma_start(out=gp[:], in_=gpack), GP, writes=[GP])
        S.dma("pool", lambda e: e.dma_start(out=cm[:], in_=cpack), CM, writes=[CM])
        S.op("pool", lambda e: e.memset(VA[:], 1.0), writes=[VAB])

        def wcv(name, sub_off, n, dst_re, src_ap, prange=None, **kw):
            off, sz = PIECES[name]
            dst = WS[:, off + sub_off: off + sub_off + n] if prange is None else WS[prange[0]:prange[1], off + sub_off: off + sub_off + n]
            if dst_re is not None:
                dst = dst.rearrange(dst_re, **kw)
            prep(lambda d=dst, s=src_ap, name=name: S.dma("pool", lambda e: e.dma_start(out=d, in_=s), WSB[name], pwrites=[WSB[name]]))

        late = []
        pmode = {"defer": False}

        def prep(rec):
            if pmode["defer"]:
                late.append(rec)
            else:
                rec()

        def prep_ffn(f):
            wg, wu, wd = w_f[f]
            for b in range(11):
                cs = slice(b * 256, (b + 1) * 256)
                wcv(f"gu{f}_{b}", 0, 2048, "p (k c) -> p k c", wg[:, cs].rearrange("(k p) c -> p k c", p=128), k=8)
                wcv(f"gu{f}_{b}", 2048, 2048, "p (k c) -> p k c", wu[:, cs].rearrange("(k p) c -> p k c", p=128), k=8)
            for i in range(8):
                wcv(f"dn{f}_{i}", 0, NJ * 128, "p (j c) -> p j c", wd[:, i * 128:(i + 1) * 128].rearrange("(j p) c -> p j c", p=128), j=NJ)

        def colblk(name, src, c0, ncols, sub_off=0, kk=8):
            wcv(name, sub_off, kk * ncols, "p (k c) -> p k c", src[:, c0:c0 + ncols].rearrange("(k p) c -> p k c", p=128), k=kk)

        prep_ffn(1)
        off_kv = PIECES["kv"][0]
        for (c0, so) in ((512, 0), (640, 128)):
            dst = WS[:, off_kv: off_kv + 2048].rearrange("p (k c) -> p k c", k=8)[:, :, so:so + 128]
            src = w_in[:, c0:c0 + 128].rearrange("(k p) c -> p k c", p=128)
            prep(lambda d=dst, s=src: S.dma("pool", lambda e: e.dma_start(out=d, in_=s), WSB["kv"], pwrites=[WSB["kv"]]))
        pmode["defer"] = True
        off_q = PIECES["q"][0]
        for j in range(4):
            for hh, hd in ((0, j), (1, 4 + j)):
                dst = WS[:, off_q: off_q + 4096].rearrange("p (k c) -> p k c", k=8)[:, :, j * 128 + hh * 64: j * 128 + hh * 64 + 64]
                src = w_in[:, hd * 64:(hd + 1) * 64].rearrange("(k p) c -> p k c", p=128)
                prep(lambda d=dst, s=src: S.dma("pool", lambda e: e.dma_start(out=d, in_=s), WSB["q"], pwrites=[WSB["q"]]))
        colblk("z", w_in, 768, 512)
        wcv("pw", 0, 512, "p (g d) -> p g d", pool_w.rearrange("g c d -> c g d"), g=4)
        wcv("upool", 0, 4096, "p (k c) -> p k c", w_up_pool.rearrange("(k p) c -> p k c", p=128), k=4)
        colblk("ga0", w_in, 1280, 512)
        colblk("ga1", w_in, 1792, 512)
        colblk("gb0", w_in, 2304, 512)
        colblk("gb1", w_in, 2816, 512)
        for j in range(4):
            for (p0, hd) in ((0, 4 + j), (64, j)):
                wcv("uattn", j * 1024, 1024, None, w_up_attn[hd * 64:(hd + 1) * 64, :], prange=(p0, p0 + 64))
        colblk("wo0", w_out, 0, 512)
        colblk("wo1", w_out, 512, 512)
        prep_ffn(2)
        colblk("pg0", w_ple_gate, 0, 512)
        colblk("pg1", w_ple_gate, 512, 512)
        wcv("pp", 0, 2048, "p (k c) -> p k c", w_ple_proj.rearrange("(k p) c -> p k c", p=128), k=2)

        seqA = [f"gu1_{b}" for b in range(11)] + [f"dn1_{i}" for i in range(8)] + ["kv"]
        seqB = (["q", "z", "pw", "ga0", "ga1", "gb0", "gb1", "uattn", "upool", "wo0", "wo1"]
                + [f"gu2_{b}" for b in range(11)] + [f"dn2_{i}" for i in range(8)] + ["pg0", "pg1"])
        wseq = [] if collect else list(wseq_in)
        wstate = {"issued": 0, "next": 0}

        def w_issue_upto(k):
            while wstate["issued"] < min(k, len(wseq)):
                i = wstate["issued"]
                name = wseq[i]
                off, sz = PIECES[name]
                s = i % NSLOT
                S.dma("sp", lambda e, s=s, off=off, sz=sz: e.dma_start(out=wsl[s][:, 0:sz], in_=WS[:, off:off + sz]),
                      WSL[s], reads=[WSB[name]], writes=[WSL[s]])
                wstate["issued"] += 1

        def wget(name, hold=0):
            i = wstate["next"]
            if collect:
                wseq.append(name)
            assert wseq[i] == name, (i, wseq[i], name)
            w_issue_upto(i + NSLOT - hold)
            wstate["next"] += 1
            s = i % NSLOT
            return wsl[s], WSL[s]

        def mm(out_ap, lhsT, rhs, start, stop, reads, wbuf):
            S.op("pe", lambda e: e.matmul(out_ap, lhsT=lhsT, rhs=rhs, start=start, stop=stop), reads=reads, writes=[wbuf])

        def rstd_from_psum(pbank, PBANK, ncols_lo, r, R, col0=0, first=True):
            sl_ = slice(col0, col0 + ncols_lo)
            rd = [PBANK, GP] + ([] if first else [R])
            S.op("act", lambda e: e.activation(out=r[:, sl_], in_=pbank[:, 0:ncols_lo], func=AF.Ln, bias=gcol(GC_EPS)),
                 reads=rd, writes=[R])
            S.op("act", lambda e: e.activation(out=r[:, sl_], in_=r[:, sl_], func=AF.Exp, scale=-0.5), reads=[R], writes=[R])

        def rms_rstd(src_chunks, src_bufs, ncols, ones_ap):
            n = 8
            for c in range(n):
                a_, b_ = src_chunks[c], src_bufs[c]
                sq, SQ = TBFn()
                eng_ = "pool" if c % 2 == 0 else "dve"
                S.op(eng_, lambda e, sq=sq, a=a_: e.tensor_tensor(out=sq[:, 0:ncols], in0=a, in1=a, op=ALU.mult),
                     reads=[b_], writes=[SQ])
                mm(ps[6][:, 0:512], ones_ap, sq[:, 0:512], c == 0, c == n - 1, [SQ, CM], PS[6])
                if ncols > 512:
                    mm(ps[7][:, 0:ncols - 512], ones_ap, sq[:, 512:ncols], c == 0, c == n - 1, [SQ, CM], PS[7])
            r, R = T32n()
            rstd_from_psum(ps[6], PS[6], 512, r, R)
            if ncols > 512:
                rstd_from_psum(ps[7], PS[7], ncols - 512, r, R, col0=512, first=False)
            return r, R

        def norm_to_ubuf(hb, HBb, gc0, ncols, dst=None, DST=None):
            dst = ubuf if dst is None else dst
            DST = UB if DST is None else DST
            r, R = rms_rstd([hb[:, c, 0:ncols] for c in range(8)], HBb, ncols, ONES)
            for c in range(8):
                S.op("dve", lambda e, c=c: e.scalar_tensor_tensor(out=dst[:, c, 0:ncols], in0=hb[:, c, 0:ncols], scalar=gcol(gc0 + c),
                                                                 in1=r[:, 0:ncols], op0=ALU.mult, op1=ALU.mult),
                     reads=[HBb[c], R, GP], writes=[DST[c]])

        def ffn(f, hb, HBb, src=None, SRC=None, side=None, nside=1):
            src = ubuf if src is None else src
            SRC = UB if SRC is None else SRC

            def side_step():
                if side is not None:
                    for _ in range(nside):
                        try:
                            next(side)
                        except StopIteration:
                            pass

            for b in range(11):
                w, Wb = wget(f"gu{f}_{b}")
                wv = w[:, 0:4096].rearrange("p (a k c) -> p a k c", a=2, k=8)
                for jj in range(2):
                    j = 2 * b + jj
                    pg, PG = ps[j % 2], PS[j % 2]
                    pu, PU = ps[2 + j % 2], PS[2 + j % 2]
                    for k in range(8):
                        mm(pg[:], wv[:, 0, k, jj * 128:(jj + 1) * 128], src[:, k, 0:512], k == 0, k == 7, [Wb, SRC[k]], PG)
                    for k in range(8):
                        mm(pu[:], wv[:, 1, k, jj * 128:(jj + 1) * 128], src[:, k, 0:512], k == 0, k == 7, [Wb, SRC[k]], PU)
                    sl, SL = slb[j % 2], SLB[j % 2]
                    S.op("act", lambda e, sl=sl, pg=pg: e.activation(out=sl[:], in_=pg[:], func=AF.Silu), reads=[PG], writes=[SL])
                    S.op("dve", lambda e, sl=sl, pu=pu, j=j: e.tensor_tensor(out=hid[:, j, :], in0=pu[:], in1=sl[:], op=ALU.mult),
                         reads=[PU, SL], writes=[HID[j]])
                    side_step()
            for i in range(8):
                w, Wb = wget(f"dn{f}_{i}")
                wv = w[:, 0:NJ * 128].rearrange("p (j c) -> p j c", j=NJ)
                py, PY = ps[4 + i % 2], PS[4 + i % 2]
                for j in range(NJ):
                    mm(py[:], wv[:, j, :], hid[:, j, :], j == 0, j == NJ - 1, [Wb, HID[j]], PY)
                S.op("dve", lambda e, i=i, py=py: e.scalar_tensor_tensor(out=hb[:, i, 0:512], in0=py[:], scalar=0.5, in1=hb[:, i, 0:512],
                                                                        op0=ALU.mult, op1=ALU.add),
                     reads=[PY, HBb[i]], writes=[HBb[i]])
                side_step()
            if side is not None:
                for _ in side:
                    pass

        def headnorm_rope(psrc, PSRC, gc, tokcol, out_ap, out_buf, pw):
            raw, RAW = T32n()
            S.op("act", lambda e: e.activation(out=raw[:, 0:512], in_=psrc[:], func=AF.Copy), reads=[PSRC], writes=[RAW])
            sq, SQ = TBFn()
            S.op("pool", lambda e: e.tensor_tensor(out=sq[:, 0:512], in0=raw[:, 0:512], in1=raw[:, 0:512], op=ALU.mult), reads=[RAW], writes=[SQ])
            mm(ps[6][:], BONES, sq[:, 0:512], True, True, [SQ, CM], PS[6])
            r, R = T32n()
            rstd_from_psum(ps[6], PS[6], 512, r, R)
            kn, KN = TBFn()
            S.op("dve", lambda e: e.scalar_tensor_tensor(out=kn[:, 0:512], in0=raw[:, 0:512], scalar=gcol(gc), in1=r[:, 0:512],
                                                         op0=ALU.mult, op1=ALU.mult), reads=[RAW, R, GP], writes=[KN])
            mm(ps[7][:], RMAT, kn[:, 0:512], True, True, [KN, CM], PS[7])
            t1, T1 = T32n()
            S.op("dve", lambda e: e.tensor_tensor(out=t1[:, 0:512], in0=kn[:, 0:512], in1=cosb[:], op=ALU.mult), reads=[KN, COS], writes=[T1])
            t2, T2 = T32n()
            S.op("dve", lambda e: e.tensor_tensor(out=t2[:, 0:512], in0=ps[7][:], in1=sinb[:], op=ALU.mult), reads=[PS[7], SIN], writes=[T2])
            if pw:
                S.op("dve", lambda e: e.tensor_tensor(out=out_ap, in0=t1[:, 0:512], in1=t2[:, 0:512], op=ALU.add), reads=[T1, T2], pwrites=[out_buf])
            else:
                S.op("dve", lambda e: e.tensor_tensor(out=out_ap, in0=t1[:, 0:512], in1=t2[:, 0:512], op=ALU.add), reads=[T1, T2], writes=[out_buf])

        def load_rope(t):
            cs = slice(t * NT, (t + 1) * NT)
            S.dma("sp", lambda e: e.dma_start(out=cosb[:], in_=cosT[:, cs]), COS, writes=[COS])
            S.dma("sp", lambda e: e.dma_start(out=sinb[:], in_=sinT[:, cs]), SIN, writes=[SIN])

        xTv = xT.rearrange("(k p) n -> p k n", p=128)
        hSv = hS.rearrange("(k p) n -> p k n", p=128)
        yTv = yT.rearrange("(k p) n -> p k n", p=128)
        pTv = pT.rearrange("(k p) n -> p k n", p=128)

        def load_x(t, bi):
            S.dma("sp", lambda e: e.dma_start(out=hbuf[bi][:, :, 0:512], in_=xTv[:, :, t * NT:(t + 1) * NT]), HB[bi][0], writes=HB[bi])

        UA = [(ubuf, UB), (ubuf2, UB2)]

        def head_gen(t, bi, dve_only=False):
            hb, HBb = hbuf[bi], HB[bi]
            dst, DST = UA[t % 2]
            sqs = []
            for c in range(8):
                sq, SQ = (tbf[c % 4], TBF[c % 4]) if c < 4 else (PT[c % 4], PTB[c % 4])
                eng_ = "pool" if (c % 2 == 0 and not dve_only) else "dve"
                S.op(eng_, lambda e, sq=sq, c=c: e.tensor_tensor(out=sq[:, 0:512], in0=hb[:, c, 0:512], in1=hb[:, c, 0:512], op=ALU.mult),
                     reads=[HBb[c]], writes=[SQ])
                sqs.append((sq, SQ))
                if c % 2 == 1:
                    yield
            yield
            for c in range(8):
                mm(ps[6][:], ONES, sqs[c][0][:, 0:512], c == 0, c == 7, [sqs[c][1], CM], PS[6])
            yield
            yield
            r, R = t32[3], T32[3]
            rstd_from_psum(ps[6], PS[6], 512, r, R)
            yield
            yield
            for c in range(8):
                S.op("dve", lambda e, c=c: e.scalar_tensor_tensor(out=dst[:, c, 0:512], in0=hb[:, c, 0:512], scalar=gcol(GC_F1 + c),
                                                                 in1=r[:, 0:512], op0=ALU.mult, op1=ALU.mult),
                     reads=[HBb[c], R, GP], writes=[DST[c]])
                if c % 4 == 3:
                    yield

        def tail_gen(t, bi, lt, after_u, dve_only=False):
            hb, HBb = hbuf[bi], HB[bi]
            dst, DST = UA[t % 2]
            load_rope(t)
            sqs = []
            for c in range(8):
                sq, SQ = (tbf[c % 4], TBF[c % 4]) if c < 4 else (PT[c % 4], PTB[c % 4])
                eng_ = "pool" if (c % 2 == 0 and not dve_only) else "dve"
                S.op(eng_, lambda e, sq=sq, c=c: e.tensor_tensor(out=sq[:, 0:512], in0=hb[:, c, 0:512], in1=hb[:, c, 0:512], op=ALU.mult),
                     reads=[HBb[c]], writes=[SQ])
                sqs.append((sq, SQ))
                if c % 2 == 1:
                    yield
            yield
            for c in range(8):
                mm(ps[6][:], ONES, sqs[c][0][:, 0:512], c == 0, c == 7, [sqs[c][1], CM], PS[6])
            yield
            yield
            r, R = t32[3], T32[3]
            rstd_from_psum(ps[6], PS[6], 512, r, R)
            yield
            yield
            for c in range(8):
                S.op("dve", lambda e, c=c: e.scalar_tensor_tensor(out=dst[:, c, 0:512], in0=hb[:, c, 0:512], scalar=gcol(GC_MIX + c),
                                                                 in1=r[:, 0:512], op0=ALU.mult, op1=ALU.mult),
                     reads=[HBb[c], R, GP], writes=[DST[c]])
                if c % 4 == 3:
                    yield
            if after_u is not None:
                after_u()
            yield
            w, Wb = wget("kv", hold=1)
            wv = w[:, 0:2048].rearrange("p (k c) -> p k c", k=8)
            for k in range(8):
                mm(ps[7][:], wv[:, k, 0:128], dst[:, k, 0:512], k == 0, k == 7, [Wb, DST[k]], PS[7])
            for s_ in range(4):
                for k in range(8):
                    mm(ps[6][:, s_ * 128:(s_ + 1) * 128], dst[:, k, s_ * 128:(s_ + 1) * 128], wv[:, k, 128:256], k == 0, k == 7, [Wb, DST[k]], PS[6])
            yield
            yield
            pv = ps[6][:].rearrange("p (s c) -> p s c", s=4)
            S.op("act", lambda e: e.activation(out=VA[:, lt * 4:(lt + 1) * 4, 64:192], in_=pv, func=AF.Copy), reads=[PS[6]], pwrites=[VAB])
            raw, RAW = t32[0], T32[0]
            S.op("act", lambda e: e.activation(out=raw[:, 0:512], in_=ps[7][:], func=AF.Copy), reads=[PS[7]], writes=[RAW])
            yield
            sq, SQ = tbf[0], TBF[0]
            S.op("dve" if dve_only else "pool", lambda e: e.tensor_tensor(out=sq[:, 0:512], in0=raw[:, 0:512], in1=raw[:, 0:512], op=ALU.mult), reads=[RAW], writes=[SQ])
            yield
            yield
            mm(ps[6][:], BONES, sq[:, 0:512], True, True, [SQ, CM], PS[6])
            yield
            yield
            r2, R2 = t32[1], T32[1]
            rstd_from_psum(ps[6], PS[6], 512, r2, R2)
            yield
            yield
            kn, KN = tbf[1], TBF[1]
            S.op("dve", lambda e: e.scalar_tensor_tensor(out=kn[:, 0:512], in0=raw[:, 0:512], scalar=gcol(GC_K), in1=r2[:, 0:512],
                                                         op0=ALU.mult, op1=ALU.mult), reads=[RAW, R2, GP], writes=[KN])
            yield
            mm(ps[7][:], RMAT, kn[:, 0:512], True, True, [KN, CM], PS[7])
            S.op("dve", lambda e: e.tensor_tensor(out=raw[:, 0:512], in0=kn[:, 0:512], in1=cosb[:], op=ALU.mult), reads=[KN, COS], writes=[RAW])
            yield
            yield
            S.op("dve", lambda e: e.tensor_tensor(out=r2[:, 0:512], in0=ps[7][:], in1=sinb[:], op=ALU.mult), reads=[PS[7], SIN], writes=[R2])
            S.op("dve", lambda e: e.tensor_tensor(out=KT[:, lt * NT:(lt + 1) * NT], in0=raw[:, 0:512], in1=r2[:, 0:512], op=ALU.add),
                 reads=[RAW, R2], pwrites=[KTB])

        def run_gen(g_):
            if g_ is not None:
                for _ in g_:
                    pass

        def chain2(*gens):
            for g_ in gens:
                if g_ is not None:
                    yield from g_

        def phaseA_run(tiles, n0, blk_base_tile, spills, final_loader):
            nT = len(tiles)
            bis = [(n0 + i) % 2 for i in range(nT)]
            run_gen(head_gen(tiles[0], bis[0], dve_only=(n0 == 0)))
            if nT > 1:
                load_x(tiles[1], bis[1])
            tail = None
            for i, t in enumerate(tiles):
                bi = bis[i]
                hb, HBb = hbuf[bi], HB[bi]
                for _ in range(min(len(late), 12)):
                    late.pop(0)()
                busy_pool = len(late) > 0 or (n0 == 0 and i < 2)
                nxt_head = head_gen(tiles[i + 1], bis[i + 1], dve_only=busy_pool) if i + 1 < nT else None
                src, SRC = UA[t % 2]
                ffn(1, hb, HBb, src=src, SRC=SRC, side=chain2(tail, nxt_head))
                if spills[i]:
                    S.dma("pool", lambda e, t=t, hb=hb: e.dma_start(out=hSv[:, :, t * NT:(t + 1) * NT], in_=hb[:, :, 0:512]), HS[t], reads=HBb, writes=[HS[t]])
                if i + 2 < nT:
                    cb = (lambda tt=tiles[i + 2], bb=bi: load_x(tt, bb))
                else:
                    cb = None
                tail = tail_gen(t, bi, t - blk_base_tile, cb, dve_only=busy_pool)
            run_gen(tail)
            if final_loader is not None:
                final_loader()

        def load_h(t, bi, base, size):
            c0 = t * NT
            lh = base + ((c0 - 8 - base) % size)
            rh = base + ((c0 + NT - base) % size)
            S.dma("sp", lambda e: e.dma_start(out=hbuf[bi][:, :, 0:512], in_=hSv[:, :, c0:c0 + NT]), HB[bi][0],
                  reads=[HS[t]], writes=HB[bi])
            S.dma("sp", lambda e: e.dma_start(out=hbuf[bi][:, :, 512:520], in_=hSv[:, :, lh:lh + 8]), HB[bi][0],
                  reads=[HS[lh // NT]], pwrites=HB[bi])
            S.dma("sp", lambda e: e.dma_start(out=hbuf[bi][:, :, 520:528], in_=hSv[:, :, rh:rh + 8]), HB[bi][0],
                  reads=[HS[rh // NT]], pwrites=HB[bi])

        def norm_side_gen(bi_n):
            hbn, HBn = hbuf[bi_n], HB[bi_n]
            for rnd in range(2):
                sqs = []
                for cc in range(4):
                    c = rnd * 4 + cc
                    sq, SQ = tbf[cc], TBF[cc]
                    eng_ = "pool" if c % 2 == 0 else "dve"
                    S.op(eng_, lambda e, sq=sq, c=c: e.tensor_tensor(out=sq[:, 0:528], in0=hbn[:, c, 0:528], in1=hbn[:, c, 0:528], op=ALU.mult),
                         reads=[HBn[c]], writes=[SQ])
                    sqs.append((sq, SQ))
                    if cc % 2 == 1:
                        yield
                yield
                for cc in range(4):
                    c = rnd * 4 + cc
                    mm(ps[6][:, 0:512], ONES, sqs[cc][0][:, 0:512], c == 0, c == 7, [sqs[cc][1], CM], PS[6])
                    mm(ps[7][:, 0:16], ONES, sqs[cc][0][:, 512:528], c == 0, c == 7, [sqs[cc][1], CM], PS[7])
                yield
                yield
            r, R = t32[3], T32[3]
            rstd_from_psum(ps[6], PS[6], 512, r, R)
            rstd_from_psum(ps[7], PS[7], 16, r, R, col0=512, first=False)
            yield
            yield
            for c in range(8):
                S.op("dve", lambda e, c=c: e.scalar_tensor_tensor(out=ubuf2[:, c, 0:528], in0=hbn[:, c, 0:528], scalar=gcol(GC_MIX + c),
                                                                 in1=r[:, 0:528], op0=ALU.mult, op1=ALU.mult),
                     reads=[HBn[c], R, GP], writes=[UB2[c]])
                if c % 4 == 3:
                    yield

        def zpool_ops(g, first, last, flcol, pz, PZ, ph, PH):
            S.op("act", lambda e: e.activation(out=zext[:, g, 8:520], in_=pz[:], func=AF.Copy), reads=[PZ], pwrites=[ZX[g]])
            if first:
                S.op("dve", lambda e: e.tensor_scalar(out=zext[:, g, 0:8], in0=ph[:, 0:8], scalar1=gcol(flcol), scalar2=None, op0=ALU.mult), reads=[PH, GP], pwrites=[ZX[g]])
            else:
                S.op("dve", lambda e: e.tensor_copy(out=zext[:, g, 0:8], in_=ph[:, 0:8]), reads=[PH], pwrites=[ZX[g]])
            if last:
                S.op("dve", lambda e: e.tensor_scalar(out=zext[:, g, 520:528], in0=ph[:, 8:16], scalar1=gcol(flcol + 1), scalar2=None, op0=ALU.mult), reads=[PH, GP], pwrites=[ZX[g]])
            else:
                S.op("dve", lambda e: e.tensor_copy(out=zext[:, g, 520:528], in_=ph[:, 8:16]), reads=[PH], pwrites=[ZX[g]])

        def pool_chain(g):
            z = zext[:, g, :]
            add = lambda o, a, b_, rd, wr: S.op("pool", lambda e: e.tensor_tensor(out=o, in0=a, in1=b_, op=ALU.add), reads=rd, writes=wr)
            add(tmpA[:, 1:528], z[:, 0:527], z[:, 1:528], [ZX[g]], [TA])
            cur, CUR, oth, OTH = tmpA, TA, tmpB, TB
            lo, hi, sh = 1, 528, 1
            for lvl in range(g):
                nlo, nhi = lo + sh, hi - sh
                add(oth[:, nlo:nhi], cur[:, nlo - sh:nhi - sh], cur[:, nlo + sh:nhi + sh], [CUR], [OTH])
                cur, CUR, oth, OTH = oth, OTH, cur, CUR
                lo, hi, sh = nlo, nhi, sh * 2
            S.op("pool", lambda e, cur=cur, oth=oth: e.tensor_tensor(out=oth[:, 8:520], in0=cur[:, 8:520], in1=invcb[:, g, :], op=ALU.mult),
                 reads=[CUR, INVC], writes=[OTH])
            S.op("pool", lambda e, oth=oth: e.tensor_tensor(out=pooled[:, g, :], in0=oth[:, 8:520], in1=zext[:, g, 8:520], op=ALU.subtract),
                 reads=[OTH, ZX[g]], writes=[PO[g]])

        def q_side_gen(t_n):
            load_rope(t_n)
            offq = PIECES["q"][0]
            zflat = zext[:].rearrange("p g c -> p (g c)").bitcast(BF16)[:, 0:4096]
            S.dma("sp", lambda e: e.dma_start(out=zflat, in_=WS[:, offq:offq + 4096]), ZX[0], reads=[WSB["q"]], writes=ZX)
            wq = zflat.rearrange("p (k c) -> p k c", k=8)
            yield
            raw, RAW = t32[0], T32[0]
            r2, R2 = t32[1], T32[1]
            sq, SQ = tbf[0], TBF[0]
            kn, KN = tbf[1], TBF[1]
            for j in range(4):
                for k in range(8):
                    mm(ps[7][:], wq[:, k, j * 128:(j + 1) * 128], ubuf2[:, k, 0:512], k == 0, k == 7, ZX + [UB2[k]], PS[7])
                yield
                yield
                yield
                S.op("act", lambda e: e.activation(out=raw[:, 0:512], in_=ps[7][:], func=AF.Copy), reads=[PS[7]], writes=[RAW])
                yield
                yield
                S.op("pool", lambda e: e.tensor_tensor(out=sq[:, 0:512], in0=raw[:, 0:512], in1=raw[:, 0:512], op=ALU.mult), reads=[RAW], writes=[SQ])
                yield
                yield
                yield
                mm(ps[6][:], BONES, sq[:, 0:512], True, True, [SQ, CM], PS[6])
                yield
                yield
                yield
                rstd_from_psum(ps[6], PS[6], 512, r2, R2)
                yield
                yield
                yield
                S.op("dve", lambda e: e.scalar_tensor_tensor(out=kn[:, 0:512], in0=raw[:, 0:512], scalar=gcol(GC_Q), in1=r2[:, 0:512],
                                                             op0=ALU.mult, op1=ALU.mult), reads=[RAW, R2, GP], writes=[KN])
                yield
                yield
                mm(ps[7][:], RMAT, kn[:, 0:512], True, True, [KN, CM], PS[7])
                S.op("dve", lambda e: e.tensor_tensor(out=raw[:, 0:512], in0=kn[:, 0:512], in1=cosb[:], op=ALU.mult), reads=[KN, COS], writes=[RAW])
                yield
                yield
                yield
                S.op("dve", lambda e: e.tensor_tensor(out=r2[:, 0:512], in0=ps[7][:], in1=sinb[:], op=ALU.mult), reads=[PS[7], SIN], writes=[R2])
                S.op("dve", lambda e, j=j: e.tensor_tensor(out=qT[:, j, :], in0=raw[:, 0:512], in1=r2[:, 0:512], op=ALU.add), reads=[RAW, R2], writes=[QT[j]])
                yield

        def zpool_side_gen(oi_n, first_n, last_n, flcol_n):
            ocs_n = slice(oi_n * NT, (oi_n + 1) * NT)
            S.dma("sp", lambda e: e.dma_start(out=invcb[:], in_=invcT[:, ocs_n].partition_broadcast(128)), INVC, writes=[INVC])
            offz = PIECES["z"][0]
            S.dma("sp", lambda e: e.dma_start(out=attnT[:], in_=WS[:, offz:offz + 2048].rearrange("p (k c) -> p k c", k=4)), AT[0],
                  reads=[WSB["z"]], writes=AT)
            for kk in range(4):
                S.dma("sp", lambda e, kk=kk: e.dma_start(out=PT[kk][:], in_=WS[:, offz + 2048 + kk * 512: offz + 2048 + (kk + 1) * 512]), PTB[kk],
                      reads=[WSB["z"]], writes=[PTB[kk]])
            offw = PIECES["pw"][0]
            S.dma("sp", lambda e: e.dma_start(out=pwv, in_=WS[:, offw:offw + 512]), RSTDE, reads=[WSB["pw"]], writes=[RSTDE])
            wpw = pwv.rearrange("p (g d) -> p g d", g=4)
            yield

            def zw(kc, g):
                if kc < 4:
                    return attnT[:, kc, g * 128:(g + 1) * 128], AT[kc]
                return PT[kc - 4][:, g * 128:(g + 1) * 128], PTB[kc - 4]

            for g in range(4):
                for k in range(8):
                    wap, WB_ = zw(k, g)
                    mm(ps[7][:], wap, ubuf2[:, k, 0:512], k == 0, k == 7, [WB_, UB2[k]], PS[7])
                for k in range(8):
                    wap, WB_ = zw(k, g)
                    mm(ps[6][:, 0:16], wap, ubuf2[:, k, 512:528], k == 0, k == 7, [WB_, UB2[k]], PS[6])
                yield
                yield
                yield
                zpool_ops(g, first_n, last_n, flcol_n, ps[7], PS[7], ps[6], PS[6])
                yield
                yield
                pool_chain(g)
                for _ in range(4 + g):
                    yield
                mm(ps[7][:], wpw[:, g, :], pooled[:, g, :], True, True, [RSTDE, PO[g]], PS[7])
                yield
                yield
                yield
                S.op("act", lambda e, g=g: e.activation(out=pooled[:, g, :], in_=ps[7][:], func=AF.Copy, scale=gcol(GC_PS + g)),
                     reads=[PS[7], GP], writes=[PO[g]])
                yield
                yield

        def phaseB(t, bi, oi, base, size, nkc, first, last, flcol, nxt, side_prev, defer_ple, pre_normed=False, norm_next=False, next_info=None):
            hb, HBb = hbuf[bi], HB[bi]
            while late:
                late.pop(0)()
            if not pre_normed:
                load_rope(t)
            ocs = slice(oi * NT, (oi + 1) * NT)
            if not pre_normed:
                S.dma("sp", lambda e: e.dma_start(out=invcb[:], in_=invcT[:, ocs].partition_broadcast(128)), INVC, writes=[INVC])
            def store_y():
                S.dma("pool", lambda e: e.dma_start(out=yTv[:, :, ocs], in_=hb[:, :, 0:512]), YS[bi], reads=HBb, pwrites=[YS[bi]], is_out=True)
            if STAGE == 1:
                store_y()
            if pre_normed:
                U_, UB_ = ubuf2, UB2
            else:
                U_, UB_ = ubuf, UB
                norm_to_ubuf(hb, HBb, GC_MIX, 528)
            if not pre_normed:
                w, Wb = wget("q")
                wv = w[:, 0:4096].rearrange("p (k c) -> p k c", k=8)
                for j in range(4):
                    pq, PQ = PSRn()
                    for k in range(8):
                        mm(pq[:], wv[:, k, j * 128:(j + 1) * 128], U_[:, k, 0:512], k == 0, k == 7, [Wb, UB_[k]], PQ)
                    headnorm_rope(pq, PQ, GC_Q, None, qT[:, j, :], QT[j], False)
            if STAGE == 30 and oi == 0:
                pass
            if not pre_normed:
                w, Wb = wget("z")
                wv = w[:, 0:4096].rearrange("p (k c) -> p k c", k=8)
                for g in range(4):
                    pz, PZ = PSRn()
                    for k in range(8):
                        mm(pz[:], wv[:, k, g * 128:(g + 1) * 128], U_[:, k, 0:512], k == 0, k == 7, [Wb, UB_[k]], PZ)
                    for k in range(8):
                        mm(ps[4][:, g * 16:(g + 1) * 16], wv[:, k, g * 128:(g + 1) * 128], U_[:, k, 512:528], k == 0, k == 7, [Wb, UB_[k]], PS[4])
                    S.op("act", lambda e, g=g, pz=pz: e.activation(out=zext[:, g, 8:520], in_=pz[:], func=AF.Copy), reads=[PZ], pwrites=[ZX[g]])
                    if first:
                        S.op("dve", lambda e, g=g: e.tensor_scalar(out=zext[:, g, 0:8], in0=ps[4][:, g * 16:g * 16 + 8], scalar1=gcol(flcol), scalar2=None, op0=ALU.mult), reads=[PS[4], GP], pwrites=[ZX[g]])
                    else:
                        S.op("dve", lambda e, g=g: e.tensor_copy(out=zext[:, g, 0:8], in_=ps[4][:, g * 16:g * 16 + 8]), reads=[PS[4]], pwrites=[ZX[g]])
                    if last:
                        S.op("dve", lambda e, g=g: e.tensor_scalar(out=zext[:, g, 520:528], in0=ps[4][:, g * 16 + 8:g * 16 + 16], scalar1=gcol(flcol + 1),
                                                                   scalar2=None, op0=ALU.mult), reads=[PS[4], GP], pwrites=[ZX[g]])
                    else:
                        S.op("dve", lambda e, g=g: e.tensor_copy(out=zext[:, g, 520:528], in_=ps[4][:, g * 16 + 8:g * 16 + 16]), reads=[PS[4]], pwrites=[ZX[g]])
                w, Wb = wget("pw")
                wpw = w[:, 0:512].rearrange("p (g d) -> p g d", g=4)
                for g in range(4):
                    z = zext[:, g, :]
                    add = lambda o, a, b_, rd, wr: S.op("pool", lambda e: e.tensor_tensor(out=o, in0=a, in1=b_, op=ALU.add), reads=rd, writes=wr)
                    add(tmpA[:, 1:528], z[:, 0:527], z[:, 1:528], [ZX[g]], [TA])
                    cur, CUR, oth, OTH = tmpA, TA, tmpB, TB
                    lo, hi, sh = 1, 528, 1
                    for lvl in range(g):
                        nlo, nhi = lo + sh, hi - sh
                        add(oth[:, nlo:nhi], cur[:, nlo - sh:nhi - sh], cur[:, nlo + sh:nhi + sh], [CUR], [OTH])
                        cur, CUR, oth, OTH = oth, OTH, cur, CUR
                        lo, hi, sh = nlo, nhi, sh * 2
                    S.op("pool", lambda e, cur=cur, oth=oth, g=g: e.tensor_tensor(out=oth[:, 8:520], in0=cur[:, 8:520], in1=invcb[:, g, :], op=ALU.mult),
                         reads=[CUR, INVC], writes=[OTH])
                    S.op("pool", lambda e, oth=oth, g=g: e.tensor_tensor(out=pooled[:, g, :], in0=oth[:, 8:520], in1=zext[:, g, 8:520], op=ALU.subtract),
                         reads=[OTH, ZX[g]], writes=[PO[g]])
                    pm, PM = PSRn()
                    mm(pm[:], wpw[:, g, :], pooled[:, g, :], True, True, [Wb, PO[g]], PM)
                    S.op("act", lambda e, g=g, pm=pm: e.activation(out=pooled[:, g, :], in_=pm[:], func=AF.Copy, scale=gcol(GC_PS + g)),
                         reads=[PM, GP], writes=[PO[g]])
            def gates_gen():
                for gi, nm in enumerate(("ga0", "ga1", "gb0", "gb1")):
                    w, Wb = wget(nm)
                    wv = w[:, 0:4096].rearrange("p (k c) -> p k c", k=8)
                    for oc in range(4):
                        idx = gi * 4 + oc
                        for k in range(8):
                            mm(ps[7][:], wv[:, k, oc * 128:(oc + 1) * 128], U_[:, k, 0:512], k == 0, k == 7, [Wb, UB_[k]], PS[7])
                        yield
                        yield
                        yield
                        sg, SG = T32n()
                        S.op("act", lambda e, sg=sg: e.activation(out=sg[:, 0:512], in_=ps[7][:], func=AF.Exp, scale=-1.0), reads=[PS[7]], writes=[SG])
                        yield
                        yield
                        S.op("dve", lambda e, sg=sg: e.tensor_scalar_add(out=sg[:, 0:512], in0=sg[:, 0:512], scalar1=1.0), reads=[SG], writes=[SG])
                        S.op("dve", lambda e, sg=sg, idx=idx: e.reciprocal(out=hid[:, idx, :], in_=sg[:, 0:512]), reads=[SG], writes=[HID[idx]])
                        yield

            def chain_gens(*gens):
                for g_ in gens:
                    if g_ is not None:
                        yield from g_

            side = chain_gens(gates_gen(), side_prev)

            def side_step():
                for _ in range(1 if nkc >= 64 else 2):
                    try:
                        next(side)
                    except StopIteration:
                        pass

            plo, PLO, phi, PHI = ps[4], PS[4], ps[5], PS[5]

            def issue_S(j, kc):
                ks = slice(kc * 128, (kc + 1) * 128)
                s_lo, SLO = PSRn()
                mm(s_lo[:], KT[0:64, ks], qT[0:64, j, :], True, True, [KTB, QT[j]], SLO)
                s_hi, SHI = PSRn()
                mm(s_hi[:], KT[64:128, ks], qT[64:128, j, :], True, True, [KTB, QT[j]], SHI)
                return s_lo, SLO, s_hi, SHI

            def epilogue1(j, c_lo, CLO, c_hi, CHI):
                S.op("dve", lambda e: e.reciprocal(out=attnT[0:64, j, :], in_=c_lo[0:64, 0:512]), reads=[CLO], writes=[AT[j]])
                S.op("dve", lambda e: e.reciprocal(out=attnT[64:128, j, :], in_=c_hi[64:128, 0:512]), reads=[CHI], pwrites=[AT[j]])

            def epilogue2(j, c_lo, CLO, c_hi, CHI):
                psw, PSW = PSRn()
                mm(psw[:], SWAPM, attnT[:, j, :], True, True, [AT[j], CM], PSW)
                S.op("dve", lambda e: e.tensor_tensor(out=attnT[0:64, j, :], in0=psw[0:64, :], in1=c_hi[0:64, 0:512], op=ALU.mult),
                     reads=[PSW, CHI], writes=[AT[j]])
                S.op("dve", lambda e: e.tensor_tensor(out=attnT[64:128, j, :], in0=psw[64:128, :], in1=c_lo[64:128, 0:512], op=ALU.mult),
                     reads=[PSW, CLO], pwrites=[AT[j]])

            pend_epi = None
            for j in range(4):
                pend = issue_S(j, 0)
                for kc in range(nkc):
                    s_lo, SLO, s_hi, SHI = pend
                    if kc + 1 < nkc:
                        pend = issue_S(j, kc + 1)
                    p_lo, PLb = PTn()
                    S.op("act", lambda e, p_lo=p_lo, s_lo=s_lo: e.activation(out=p_lo[:], in_=s_lo[:], func=AF.Exp, scale=0.125), reads=[SLO], writes=[PLb])
                    p_hi, PHb = PTn()
                    S.op("act", lambda e, p_hi=p_hi, s_hi=s_hi: e.activation(out=p_hi[:], in_=s_hi[:], func=AF.Exp, scale=0.125), reads=[SHI], writes=[PHb])
                    mm(plo[:], VA[:, kc, 0:128], p_lo[:], kc == 0, kc == nkc - 1, [VAB, PLb], PLO)
                    mm(phi[:], VA[:, kc, 128:256], p_hi[:], kc == 0, kc == nkc - 1, [VAB, PHb], PHI)
                    if kc == 2 and pend_epi is not None:
                        epilogue1(*pend_epi)
                    if kc == 14 and pend_epi is not None:
                        epilogue2(*pend_epi)
                        pend_epi = None
                    side_step()
                c_lo, CLO = tmpA, TA
                S.op("act", lambda e, c_lo=c_lo: e.activation(out=c_lo[:, 0:512], in_=plo[:], func=AF.Copy), reads=[PLO], writes=[CLO])
                c_hi, CHI = tmpB, TB
                S.op("act", lambda e, c_hi=c_hi: e.activation(out=c_hi[:, 0:512], in_=phi[:], func=AF.Copy), reads=[PHI], writes=[CHI])
                pend_epi = (j, c_lo, CLO, c_hi, CHI)
            epilogue1(*pend_epi)
            for _ in side:
                pass
            epilogue2(*pend_epi)
            if nxt is not None:
                nxt()
            if STAGE == 30 and oi == 0:
                dbg_put(6, attnT[:, 0, :], AT[0])
                dbg_put(0, attnT[:, 1, :], AT[1])
                dbg_put(1, attnT[:, 2, :], AT[2])
                dbg_put(2, attnT[:, 3, :], AT[3])
            wa, WAb = wget("uattn")
            wav = wa[:, 0:4096].rearrange("p (j c) -> p j c", j=4)
            wp, WPb = wget("upool", hold=1)
            wpv = wp[:, 0:4096].rearrange("p (j c) -> p j c", j=4)
            for c in range(8):
                pa, PA = ps[c % 2], PS[c % 2]
                pbb, PBB = ps[2 + c % 2], PS[2 + c % 2]
                for j in range(4):
                    mm(pa[:], wav[:, j, c * 128:(c + 1) * 128], attnT[:, j, :], j == 0, j == 3, [WAb, AT[j]], PA)
                for j in range(4):
                    mm(pbb[:], wpv[:, j, c * 128:(c + 1) * 128], pooled[:, j, :], j == 0, j == 3, [WPb, PO[j]], PBB)
                t1, T1 = T32n()
                S.op("dve", lambda e, t1=t1, pa=pa, c=c: e.tensor_tensor(out=t1[:, 0:512], in0=pa[:], in1=hid[:, c, :], op=ALU.mult), reads=[PA, HID[c]], writes=[T1])
                if STAGE == 30 and oi == 0 and c == 0:
                    dbg_put(3, pa[:], PA)
                    dbg_put(4, hid[:, 0, :], HID[0])
                    dbg_put(5, t1[:, 0:512], T1)
                    dbg_flush()
                t2, T2 = T32n()
                S.op("dve", lambda e, t2=t2, pbb=pbb, c=c: e.tensor_tensor(out=t2[:, 0:512], in0=pbb[:], in1=hid[:, 8 + c, :], op=ALU.mult), reads=[PBB, HID[8 + c]], writes=[T2])
                if STAGE == 21:
                    S.op("pool", lambda e, t1=t1, t2=t2, c=c: e.tensor_copy(out=ubuf[:, c, 0:512], in_=t1[:, 0:512]), reads=[T1, T2], writes=[UB[c]])
                elif STAGE == 22:
                    S.op("pool", lambda e, t1=t1, t2=t2, c=c: e.tensor_copy(out=ubuf[:, c, 0:512], in_=t2[:, 0:512]), reads=[T1, T2], writes=[UB[c]])
                else:
                    S.op("pool", lambda e, t1=t1, t2=t2, c=c: e.tensor_tensor(out=ubuf[:, c, 0:512], in0=t1[:, 0:512], in1=t2[:, 0:512], op=ALU.add), reads=[T1, T2], writes=[UB[c]])
            prev_sq = None
            for half in range(2):
                w, Wb = wget(f"wo{half}")
                wv = w[:, 0:4096].rearrange("p (k c) -> p k c", k=8)
                for ii in range(4):
                    i = half * 4 + ii
                    py, PY = ps[4 + i % 2], PS[4 + i % 2]
                    for k in range(8):
                        mm(py[:], wv[:, k, ii * 128:(ii + 1) * 128], ubuf[:, k, 0:512], k == 0, k == 7, [Wb, UB[k]], PY)
                    if prev_sq is not None:
                        mm(ps[6][:], ONES, prev_sq[0][:, 0:512], i == 1, False, [prev_sq[1], CM], PS[6])
                    S.op("dve", lambda e, i=i, py=py: e.tensor_tensor(out=hb[:, i, 0:512], in0=py[:], in1=hb[:, i, 0:512], op=ALU.add), reads=[PY, HBb[i]], writes=[HBb[i]])
                    sq, SQ = TBFn()
                    S.op("pool" if i % 2 == 0 else "dve", lambda e, sq=sq, i=i: e.tensor_tensor(out=sq[:, 0:512], in0=hb[:, i, 0:512], in1=hb[:, i, 0:512], op=ALU.mult),
                         reads=[HBb[i]], writes=[SQ])
                    prev_sq = (sq, SQ)
            mm(ps[6][:], ONES, prev_sq[0][:, 0:512], False, True, [prev_sq[1], CM], PS[6])
            if STAGE in (2, 21, 22):
                store_y()
            if STAGE == 0:
                rf, RF = T32n()
                rstd_from_psum(ps[6], PS[6], 512, rf, RF)
                for c in range(8):
                    S.op("dve", lambda e, c=c: e.scalar_tensor_tensor(out=ubuf[:, c, 0:512], in0=hb[:, c, 0:512], scalar=gcol(GC_F2 + c),
                                                                     in1=rf[:, 0:512], op0=ALU.mult, op1=ALU.mult),
                         reads=[HBb[c], RF, GP], writes=[UB[c]])
            else:
                norm_to_ubuf(hb, HBb, GC_F2, 512)
            if norm_next:
                side2 = chain2(norm_side_gen(1 - bi), q_side_gen(next_info[4]), zpool_side_gen(*next_info[0:4]))
            else:
                side2 = None
            ffn(2, hb, HBb, side=side2, nside=5)
            if STAGE == 3:
                store_y()
            def ple_gen():
                if not ppstate["loaded"]:
                    offp, szp = PIECES["pp"]
                    S.dma("sp", lambda e: e.dma_start(out=ppb[:], in_=WS[:, offp:offp + szp]), PPB, reads=[WSB["pp"]], writes=[PPB])
                    ppstate["loaded"] = True
                S.dma("pool", lambda e: e.dma_start(out=pb[:], in_=pTv[:, :, ocs]), PB, writes=[PB])
                wvp = ppb[:].rearrange("p (k c) -> p k c", k=2)
                yield

                def eproj(i):
                    for k in range(2):
                        mm(ps[7][:], wvp[:, k, i * 128:(i + 1) * 128], pb[:, k, :], k == 0, k == 1, [PPB, PB], PS[7])

                prev = None
                for i in range(8):
                    eproj(i)
                    if prev is not None:
                        mm(ps[6][:], ONES, prev[0][:, 0:512], i == 1, False, [prev[1], CM], PS[6])
                    yield
                    yield
                    yield
                    e32, E32 = T32n()
                    S.op("act", lambda e, e32=e32: e.activation(out=e32[:, 0:512], in_=ps[7][:], func=AF.Copy), reads=[PS[7]], writes=[E32])
                    sq, SQ = TBFn()
                    S.op("pool", lambda e, sq=sq, e32=e32: e.tensor_tensor(out=sq[:, 0:512], in0=e32[:, 0:512], in1=e32[:, 0:512], op=ALU.mult),
                         reads=[E32], writes=[SQ])
                    prev = (sq, SQ)
                    yield
                    yield
                    yield
                mm(ps[6][:], ONES, prev[0][:, 0:512], False, True, [prev[1], CM], PS[6])
                yield
                yield
                yield
                rstd_from_psum(ps[6], PS[6], 512, rstde, RSTDE)
                yield
                prev = None
                for c in range(8):
                    sq, SQ = TBFn()
                    S.op("pool", lambda e, sq=sq, c=c: e.tensor_tensor(out=sq[:, 0:512], in0=hb[:, c, 0:512], in1=hb[:, c, 0:512], op=ALU.mult),
                         reads=[HBb[c]], writes=[SQ])
                    if prev is not None:
                        mm(ps[6][:], ONES, prev[0][:, 0:512], c == 1, False, [prev[1], CM], PS[6])
                    prev = (sq, SQ)
                    yield
                    yield
                    yield
                mm(ps[6][:], ONES, prev[0][:, 0:512], False, True, [prev[1], CM], PS[6])
                yield
                yield
                yield
                r3, R3 = T32n()
                rstd_from_psum(ps[6], PS[6], 512, r3, R3)
                yield
                yield
                for c in range(8):
                    S.op("dve", lambda e, c=c: e.scalar_tensor_tensor(out=ubuf2[:, c, 0:512], in0=hb[:, c, 0:512], scalar=gcol(GC_PG + c),
                                                                     in1=r3[:, 0:512], op0=ALU.mult, op1=ALU.mult),
                         reads=[HBb[c], R3, GP], writes=[UB2[c]])
                yield
                for half in range(2):
                    w, Wb = wget(f"pg{half}")
                    wv = w[:, 0:4096].rearrange("p (k c) -> p k c", k=8)
                    for ii in range(4):
                        i = half * 4 + ii
                        for k in range(8):
                            mm(ps[7][:], wv[:, k, ii * 128:(ii + 1) * 128], ubuf2[:, k, 0:512], k == 0, k == 7, [Wb, UB2[k]], PS[7])
                        yield
                        yield
                        yield
                        gt, GT = T32n()
                        S.op("act", lambda e, gt=gt: e.activation(out=gt[:, 0:512], in_=ps[7][:], func=AF.Exp, scale=-1.0), reads=[PS[7]], writes=[GT])
                        yield
                        yield
                        S.op("dve", lambda e, gt=gt: e.tensor_scalar_add(out=gt[:, 0:512], in0=gt[:, 0:512], scalar1=1.0), reads=[GT], writes=[GT])
                        S.op("dve", lambda e, gt=gt: e.reciprocal(out=gt[:, 0:512], in_=gt[:, 0:512]), reads=[GT], writes=[GT])
                        eproj(i)
                        yield
                        yield
                        yield
                        t1, T1 = T32n()
                        S.op("dve", lambda e, t1=t1, i=i: e.scalar_tensor_tensor(out=t1[:, 0:512], in0=ps[7][:], scalar=gcol(GC_PP + i), in1=rstde[:, 0:512],
                                                                                op0=ALU.mult, op1=ALU.mult), reads=[PS[7], RSTDE, GP], writes=[T1])
                        S.op("pool", lambda e, t1=t1, gt=gt: e.tensor_tensor(out=t1[:, 0:512], in0=t1[:, 0:512], in1=gt[:, 0:512], op=ALU.mult), reads=[T1, GT], writes=[T1])
                        S.op("dve", lambda e, t1=t1, i=i: e.tensor_tensor(out=hb[:, i, 0:512], in0=hb[:, i, 0:512], in1=t1[:, 0:512], op=ALU.add), reads=[T1, HBb[i]], writes=[HBb[i]])
                        yield
                if STAGE == 0:
                    store_y()

            g_ = ple_gen()
            if defer_ple:
                return g_
            for _ in g_:
                pass
            return None

        plan = []
        for t in range(16):
            plan.append(("A", t, 0, t < 8 or t in (8, 15)))
        for oi in range(8):
            plan.append(("B", oi, oi, 0, TP, 64, oi == 0, oi == 7, GC_FL))
        for t in range(16, 24):
            plan.append(("A", t, 16, t < 20 or t in (20, 23)))
        for oi in range(8, 12):
            plan.append(("B", 8 + oi, oi, TP, TS, 32, oi == 8, oi == 11, GC_FL + 2))

        def mk_loader(step, bi):
            if step[0] == "A":
                return lambda: load_x(step[1], bi)
            return lambda: load_h(step[1], bi, step[3], step[4])

        mk_loader(plan[0], 0)()
        side_prev = None
        n = 0
        while n < len(plan):
            step = plan[n]
            if step[0] == "A":
                m = n
                while m < len(plan) and plan[m][0] == "A":
                    m += 1
                tiles = [plan[q][1] for q in range(n, m)]
                spills = [plan[q][3] for q in range(n, m)]
                fl = mk_loader(plan[m], m % 2) if m < len(plan) else None
                phaseA_run(tiles, n, step[2], spills, fl)
                n = m
            else:
                bi = n % 2
                nxt = mk_loader(plan[n + 1], 1 - bi) if n + 1 < len(plan) else None
                defer = n + 1 < len(plan) and plan[n + 1][0] == "B"
                pre = n > 0 and plan[n - 1][0] == "B"
                side_prev = phaseB(step[1], bi, step[2], step[3], step[4], step[5], step[6], step[7], step[8], nxt, side_prev, defer,
                                   pre_normed=pre, norm_next=defer,
                                   next_info=((plan[n + 1][2], plan[n + 1][6], plan[n + 1][7], plan[n + 1][8], plan[n + 1][1]) if defer else None))
                n += 1
        if collect:
            return list(wseq)
        assert wstate["next"] == len(wseq), (wstate, len(wseq))
        with nc.allow_low_precision(reason="bf16 matmul operands (reference tolerance is bf16-level)"):
            S.emit()
    return nc


def _rope_tables(T, shift):
    t = (np.arange(T) + shift) % T
    row = (t // 64).astype(np.float64)
    col = (t % 64).astype(np.float64)
    inv = 10000.0 ** (-np.arange(0, 32, 2, dtype=np.float64) / 32.0)
    p = np.arange(128)
    d = p % 64
    axis = d // 32
    j = d % 16
    pos = np.where(axis[:, None] == 0, row[None, :], col[None, :])
    ang = pos.astype(np.float32) * inv.astype(np.float32)[j][:, None]
    return np.cos(ang).astype(np.float32), np.sin(ang).astype(np.float32)


def _invc(T, t0, n):
    t = np.arange(t0, t0 + n)
    out = np.zeros((4, n), np.float32)
    for g, w in enumerate((2, 4, 8, 16)):
        left = w // 2
        right = w - 1 - left
        lo = np.maximum(t - left, 0)
        hi = np.minimum(t + right, T - 1)
        out[g] = 1.0 / (hi - lo + 1).astype(np.float32)
    return out


def _cpack():
    c = np.zeros((128, 512), np.float32)
    c[:, 0:128] = 1.0 / 1024.0
    c[0:64, 128:192] = 1.0 / 64.0
    c[64:128, 192:256] = 1.0 / 64.0
    for m in range(128):
        if (m % 32) < 16:
            c[m + 16, 256 + m] = -1.0
        else:
            c[m - 16, 256 + m] = 1.0
    for m in range(128):
        c[(m + 64) % 128, 384 + m] = 1.0
    return c


_NC_CACHE = {}


def kernel(x_prompt, x_sample, p_prompt, p_sample, ffn1_norm, ffn1_w_gate, ffn1_w_up, ffn1_w_down,
           mix_norm, w_in, q_norm, k_norm, w_up_attn, pool_w, pool_scale, w_up_pool, w_out,
           ffn2_norm, ffn2_w_gate, ffn2_w_up, ffn2_w_down, ple_gate_norm, w_ple_gate, w_ple_proj,
           ple_post_norm):
    f32 = lambda a: np.ascontiguousarray(np.asarray(a, dtype=np.float32))
    x_prompt, x_sample, p_prompt, p_sample = map(f32, (x_prompt, x_sample, p_prompt, p_sample))
    if "nc" not in _NC_CACHE:
        _NC_CACHE["nc"] = build_nc(build_nc(None))
    nc = _NC_CACHE["nc"]
    col8 = lambda g: f32(g).reshape(8, 128).T
    shared = {
        "ffn1_w_gate": f32(ffn1_w_gate[0]), "ffn1_w_up": f32(ffn1_w_up[0]), "ffn1_w_down": f32(ffn1_w_down[0]),
        "ffn2_w_gate": f32(ffn2_w_gate[0]), "ffn2_w_up": f32(ffn2_w_up[0]), "ffn2_w_down": f32(ffn2_w_down[0]),
        "w_in": f32(w_in[0]), "w_up_attn": f32(w_up_attn[0]), "pool_w": f32(pool_w[0]), "w_up_pool": f32(w_up_pool[0]),
        "w_out": f32(w_out[0]), "w_ple_gate": f32(w_ple_gate[0]), "w_ple_proj": f32(w_ple_proj[0]),
        "cpack": _cpack(),
    }
    gbase = np.zeros((128, GC_N), np.float32)
    gbase[:, GC_F1:GC_F1 + 8] = col8(ffn1_norm[0])
    gbase[:, GC_MIX:GC_MIX + 8] = col8(mix_norm[0])
    gbase[:, GC_F2:GC_F2 + 8] = col8(ffn2_norm[0])
    gbase[:, GC_PG:GC_PG + 8] = col8(ple_gate_norm[0])
    gbase[:, GC_PP:GC_PP + 8] = col8(ple_post_norm[0])
    gbase[:, GC_PS:GC_PS + 4] = f32(pool_scale[0]).reshape(4, 128).T
    gbase[:, GC_Q] = np.tile(f32(q_norm[0]), 2)
    gbase[:, GC_K] = np.tile(f32(k_norm[0]), 2)
    gbase[:, GC_EPS] = EPS
    in_maps = []
    for c in range(8):
        i, half = c // 2, c % 2
        sp, ss = half * (TP // 2), half * (TS // 2)
        xp = np.roll(x_prompt[i], -sp, axis=0)
        xs = np.roll(x_sample[i], -ss, axis=0)
        xT = np.ascontiguousarray(np.concatenate([xp, xs], axis=0).T)
        pT = np.ascontiguousarray(np.concatenate([p_prompt[0, i, sp:sp + TP // 2], p_sample[0, i, ss:ss + TS // 2]], axis=0).T)
        cp, sn = _rope_tables(TP, sp)
        cs_, ss_ = _rope_tables(TS, ss)
        g = gbase.copy()
        g[:, GC_FL + 0] = 1.0 if half == 1 else 0.0
        g[:, GC_FL + 1] = 1.0 if half == 0 else 0.0
        g[:, GC_FL + 2] = 1.0 if half == 1 else 0.0
        g[:, GC_FL + 3] = 1.0 if half == 0 else 0.0
        m = dict(shared)
        m.update({
            "xT": xT, "pT": pT,
            "cosT": np.ascontiguousarray(np.concatenate([cp, cs_], axis=1)),
            "sinT": np.ascontiguousarray(np.concatenate([sn, ss_], axis=1)),
            "invcT": np.ascontiguousarray(np.concatenate([_invc(TP, sp, TP // 2), _invc(TS, ss, TS // 2)], axis=1)),
            "gpack": g,
        })
        in_maps.append(m)
    res = run_bass_kernel_spmd(nc, in_maps, core_ids=list(range(8)))
    if STAGE == 30:
        np.save("_dbgT.npy", np.asarray(res.results[0]["dbgT"]))
    y_prompt = np.empty((4, TP, D), np.float32)
    y_sample = np.empty((4, TS, D), np.float32)
    for c in range(8):
        i, half = c // 2, c % 2
        sp, ss = half * (TP // 2), half * (TS // 2)
        yT = np.asarray(res.results[c]["yT"])
        y_prompt[i, sp:sp + TP // 2] = yT[:, :TP // 2].T
        y_sample[i, ss:ss + TS // 2] = yT[:, TP // 2:].T
    return (y_prompt, y_sample)
```

```python
import numpy as np
from contextlib import ExitStack
import concourse.bass as bass
import concourse.mybir as mybir
from concourse.bass_utils import run_bass_kernel_spmd

F32 = mybir.dt.float32
BF16 = mybir.dt.bfloat16
ALU = mybir.AluOpType
AF = mybir.ActivationFunctionType

D = 1024
DFF = 2816
NJ = 22
TP, TS = 8192, 4096
NT = 512
NLOC = TP + TS
NOWN = NLOC // 2
EPS = 1e-6
SLOT = 4096
import os
STAGE = int(os.environ.get("KDEBUG_STAGE", "0"))


class Buf:
    __slots__ = ("name", "writers", "readers", "sem", "ndma")

    def __init__(self, name):
        self.name = name
        self.writers = []
        self.readers = []
        self.sem = None
        self.ndma = 0


class Op:
    __slots__ = ("eng", "fn", "deps", "signal", "val", "sem", "is_dma", "dmabuf")

    def __init__(self, eng, fn, is_dma=False, dmabuf=None):
        self.eng = eng
        self.fn = fn
        self.deps = []
        self.signal = False
        self.val = None
        self.sem = None
        self.is_dma = is_dma
        self.dmabuf = dmabuf


ENGS = ("pe", "act", "dve", "pool", "sp")


def _push(lst, op):
    if not op.is_dma:
        for i, o in enumerate(lst):
            if (not o.is_dma) and o.eng == op.eng:
                lst[i] = op
                return
    lst.append(op)


class Sched:
    def __init__(self, nc, stack):
        self.nc = nc
        self.stack = stack
        self.ops = {e: [] for e in ENGS}
        self.out_dma_ops = []

    def buf(self, name):
        return Buf(name)

    def bufs(self, name, n):
        return [Buf(f"{name}{i}") for i in range(n)]

    def _add(self, op, reads, writes, pwrites):
        deps = {}
        for b in reads:
            for w in b.writers:
                deps[id(w)] = w
        for b in writes:
            for w in b.writers:
                deps[id(w)] = w
            for r in b.readers:
                deps[id(r)] = r
        for b in pwrites:
            for r in b.readers:
                deps[id(r)] = r
            for w in b.writers:
                if (not w.is_dma) and (op.is_dma or w.eng != op.eng):
                    deps[id(w)] = w
        for d in deps.values():
            if d is op:
                continue
            if (not d.is_dma) and (not op.is_dma) and d.eng == op.eng and op.eng == "pe":
                continue
            op.deps.append(d)
        for b in reads:
            _push(b.readers, op)
        for b in writes:
            b.writers = [op]
            b.readers = []
        for b in pwrites:
            _push(b.writers, op)
            b.readers = []
        self.ops[op.eng].append(op)
        return op

    def op(self, eng, fn, reads=(), writes=(), pwrites=()):
        return self._add(Op(eng, fn), list(reads), list(writes), list(pwrites))

    def dma(self, eng, fn, sembuf, reads=(), writes=(), pwrites=(), is_out=False):
        op = Op(eng, fn, is_dma=True, dmabuf=sembuf)
        self._add(op, list(reads), list(writes), list(pwrites))
        if is_out:
            self.out_dma_ops.append(op)
        return op

    def emit(self):
        nc = self.nc
        for e in ENGS:
            for op in self.ops[e]:
                for d in op.deps:
                    d.signal = True
        esem = {e: self.stack.enter_context(nc.semaphore(f"s_{e}")) for e in ENGS}
        for e in ENGS:
            cnt = 0
            for op in self.ops[e]:
                if op.is_dma:
                    b = op.dmabuf
                    if b.sem is None:
                        b.sem = self.stack.enter_context(nc.semaphore(f"d_{b.name}"))
                    b.ndma += 1
                    op.sem = b.sem
                    op.val = 16 * b.ndma
                elif op.signal:
                    cnt += 1
                    op.sem = esem[e]
                    op.val = cnt
        block = self.stack.enter_context(nc.Block())

        def run(e, eng):
            seen = {}
            for op in self.ops[e]:
                need = {}
                for d in op.deps:
                    k = id(d.sem)
                    if seen.get(k, 0) >= d.val:
                        continue
                    if k not in need or need[k][1] < d.val:
                        need[k] = (d.sem, d.val)
                for k, (s, v) in need.items():
                    eng.wait_ge(s, v)
                    seen[k] = v
                ins = op.fn(eng)
                if op.is_dma:
                    ins.then_inc(op.sem, 16)
                elif op.signal:
                    ins.then_inc(op.sem, 1)
            if e == "sp":
                for op in self.out_dma_ops:
                    k = id(op.sem)
                    if seen.get(k, 0) < op.val:
                        eng.wait_ge(op.sem, op.val)
                        seen[k] = op.val

        @block.tensor
        def _(eng):
            run("pe", eng)

        @block.scalar
        def _(eng):
            run("act", eng)

        @block.vector
        def _(eng):
            run("dve", eng)

        @block.gpsimd
        def _(eng):
            run("pool", eng)

        @block.sync
        def _(eng):
            run("sp", eng)


def piece_table():
    names = []
    for f in (1, 2):
        for b in range(11):
            names.append((f"gu{f}_{b}", 4096))
        for i in range(8):
            names.append((f"dn{f}_{i}", NJ * 128))
    names += [("kv", 8 * 256), ("q", 4096), ("z", 4096), ("pw", 512), ("upool", 4096),
              ("ga0", 4096), ("ga1", 4096), ("gb0", 4096), ("gb1", 4096), ("uattn", 4096),
              ("wo0", 4096), ("wo1", 4096), ("pg0", 4096), ("pg1", 4096), ("pp", 2048)]
    off = {}
    o = 0
    for n, sz in names:
        off[n] = (o, sz)
        o += sz
    return off, o


PIECES, WS_TOT = piece_table()

GC_F1, GC_MIX, GC_F2, GC_PG, GC_PP, GC_PS, GC_Q, GC_K, GC_FL, GC_EPS = 0, 8, 16, 24, 32, 40, 44, 45, 46, 50
GC_N = 51


def build_nc(wseq_in=None):
    collect = wseq_in is None
    nc = bass.Bass("TRN2", target_bir_lowering=False)
    dt_in = lambda n, s: nc.dram_tensor(n, s, F32, kind="ExternalInput").ap()
    xT = dt_in("xT", [D, NLOC])
    pT = dt_in("pT", [256, NOWN])
    cosT = dt_in("cosT", [128, NLOC])
    sinT = dt_in("sinT", [128, NLOC])
    invcT = dt_in("invcT", [4, NOWN])
    gpack = dt_in("gpack", [128, GC_N])
    cpack = dt_in("cpack", [128, 512])
    w_f = {}
    for f in (1, 2):
        w_f[f] = (dt_in(f"ffn{f}_w_gate", [D, DFF]), dt_in(f"ffn{f}_w_up", [D, DFF]), dt_in(f"ffn{f}_w_down", [DFF, D]))
    w_in = dt_in("w_in", [D, 3328])
    w_up_attn = dt_in("w_up_attn", [512, D])
    pool_w = dt_in("pool_w", [4, 128, 128])
    w_up_pool = dt_in("w_up_pool", [512, D])
    w_out = dt_in("w_out", [D, D])
    w_ple_gate = dt_in("w_ple_gate", [D, D])
    w_ple_proj = dt_in("w_ple_proj", [256, D])
    yT = nc.dram_tensor("yT", [D, NOWN], F32, kind="ExternalOutput").ap()
    dbgT = nc.dram_tensor("dbgT", [128, 7, 512], F32, kind="ExternalOutput").ap() if STAGE == 30 else None
    WS = nc.dram_tensor("WS", [128, WS_TOT], BF16, kind="Internal").ap()
    hS = nc.dram_tensor("hS", [D, NLOC], F32, kind="Internal").ap()

    with ExitStack() as st:
        S = Sched(nc, st)
        sb = lambda n, shp, dt: st.enter_context(nc.sbuf_tensor(n, shp, dt))
        hbuf = [sb(f"hbuf{i}", [128, 8, 528], F32) for i in range(2)]
        ubuf = sb("ubuf", [128, 8, 528], BF16)
        ubuf2 = sb("ubuf2", [128, 8, 528], BF16)
        slb = [sb(f"slb{i}", [128, 512], BF16) for i in range(2)]
        hid = sb("hid", [128, NJ, 512], BF16)
        qT = sb("qT", [128, 4, 512], BF16)
        attnT = sb("attnT", [128, 4, 512], BF16)
        pooled = sb("pooled", [128, 4, 512], BF16)
        zext = sb("zext", [128, 4, 528], F32)
        tmpA = sb("tmpA", [128, 528], F32)
        tmpB = sb("tmpB", [128, 528], F32)
        NPT = 4
        PT = [sb(f"PT{i}", [128, 512], BF16) for i in range(NPT)]
        N32, NBF = 4, 4
        t32 = [sb(f"t32_{i}", [128, 528], F32) for i in range(N32)]
        tbf = [sb(f"tbf_{i}", [128, 528], BF16) for i in range(NBF)]
        cosb = sb("cosb", [128, 512], F32)
        sinb = sb("sinb", [128, 512], F32)
        invcb = sb("invcb", [128, 4, 512], F32)
        NSLOT = 3
        wsl = [sb(f"wsl{i}", [128, SLOT], BF16) for i in range(NSLOT)]
        KT = sb("KT", [128, TP], BF16)
        VA = sb("VA", [128, TP // 128, 256], BF16)
        gp = sb("gp", [128, GC_N], F32)
        cm = sb("cm", [128, 512], BF16)
        pb = sb("pb", [128, 2, 512], BF16)
        rstde = sb("rstde", [128, 512], F32)
        ppb = sb("ppb", [128, 2048], BF16)
        ppstate = {"loaded": False}
        pwv = rstde[:].bitcast(BF16)[:, 0:512]
        ps = [st.enter_context(nc.psum_tensor(f"ps{i}", [128, 512], F32)) for i in range(8)]

        HB = [S.bufs(f"HB{i}_", 8) for i in range(2)]
        UB = S.bufs("UB", 8)
        UB2 = S.bufs("UB2", 8)
        SLB = S.bufs("SLB", 2)
        HID = S.bufs("HID", NJ)
        QT = S.bufs("QT", 4)
        AT = S.bufs("AT", 4)
        PO = S.bufs("PO", 4)
        ZX = S.bufs("ZX", 4)
        TA, TB = S.buf("TA"), S.buf("TB")
        PTB = S.bufs("PTB", NPT)
        T32 = S.bufs("T32_", N32)
        TBF = S.bufs("TBF_", NBF)
        COS, SIN, INVC = S.buf("COS"), S.buf("SIN"), S.buf("INVC")
        WSL = S.bufs("WSL", NSLOT)
        KTB, VAB = S.buf("KTB"), S.buf("VAB")
        GP, CM, PB = S.buf("GP"), S.buf("CM"), S.buf("PB")
        RSTDE = S.buf("RSTDE")
        PPB = S.buf("PPB")
        PS = S.bufs("PSB", 8)
        WSB = {n: S.buf("WS_" + n) for n in PIECES}
        HS = S.bufs("HS", NLOC // NT)
        YS = S.bufs("YS", 2)

        cnt = {"t32": 0, "tbf": 0, "pt": 0, "psr": 0}
        if STAGE == 30:
            dbgb = sb("dbgb", [128, 7, 512], F32)
            DBG = S.buf("DBG")

            def dbg_put(slot, src_ap, src_buf):
                S.op("dve", lambda e: e.tensor_copy(out=dbgb[:, slot, :], in_=src_ap), reads=[src_buf], pwrites=[DBG])

            def dbg_flush():
                S.dma("pool", lambda e: e.dma_start(out=dbgT, in_=dbgb[:]), DBG, reads=[DBG], is_out=True)

        def T32n():
            i = cnt["t32"] % N32
            cnt["t32"] += 1
            return t32[i], T32[i]

        def TBFn():
            i = cnt["tbf"] % NBF
            cnt["tbf"] += 1
            return tbf[i], TBF[i]

        def PTn():
            i = cnt["pt"] % NPT
            cnt["pt"] += 1
            return PT[i], PTB[i]

        def PSRn():
            i = cnt["psr"] % 4
            cnt["psr"] += 1
            return ps[i], PS[i]

        gcol = lambda c: gp[:, c:c + 1]
        ONES = cm[:, 0:128]
        BONES = cm[:, 128:256]
        RMAT = cm[:, 256:384]
        SWAPM = cm[:, 384:512]

        S.dma("sp", lambda e: e.dma_start(out=gp[:], in_=gpack), GP, writes=[GP])
        S.dma("pool", lambda e: e.dma_start(out=cm[:], in_=cpack), CM, writes=[CM])
        S.op("pool", lambda e: e.memset(VA[:], 1.0), writes=[VAB])

        def wcv(name, sub_off, n, dst_re, src_ap, prange=None, **kw):
            off, sz = PIECES[name]
            dst = WS[:, off + sub_off: off + sub_off + n] if prange is None else WS[prange[0]:prange[1], off + sub_off: off + sub_off + n]
            if dst_re is not None:
                dst = dst.rearrange(dst_re, **kw)
            prep(lambda d=dst, s=src_ap, name=name: S.dma("pool", lambda e: e.dma_start(out=d, in_=s), WSB[name], pwrites=[WSB[name]]))

        late = []
        pmode = {"defer": False}

        def prep(rec):
            if pmode["defer"]:
                late.append(rec)
            else:
                rec()

        def prep_ffn(f):
            wg, wu, wd = w_f[f]
            for b in range(11):
                cs = slice(b * 256, (b + 1) * 256)
                wcv(f"gu{f}_{b}", 0, 2048, "p (k c) -> p k c", wg[:, cs].rearrange("(k p) c -> p k c", p=128), k=8)
                wcv(f"gu{f}_{b}", 2048, 2048, "p (k c) -> p k c", wu[:, cs].rearrange("(k p) c -> p k c", p=128), k=8)
            for i in range(8):
                wcv(f"dn{f}_{i}", 0, NJ * 128, "p (j c) -> p j c", wd[:, i * 128:(i + 1) * 128].rearrange("(j p) c -> p j c", p=128), j=NJ)

        def colblk(name, src, c0, ncols, sub_off=0, kk=8):
            wcv(name, sub_off, kk * ncols, "p (k c) -> p k c", src[:, c0:c0 + ncols].rearrange("(k p) c -> p k c", p=128), k=kk)

        prep_ffn(1)
        off_kv = PIECES["kv"][0]
        for (c0, so) in ((512, 0), (640, 128)):
            dst = WS[:, off_kv: off_kv + 2048].rearrange("p (k c) -> p k c", k=8)[:, :, so:so + 128]
            src = w_in[:, c0:c0 + 128].rearrange("(k p) c -> p k c", p=128)
            prep(lambda d=dst, s=src: S.dma("pool", lambda e: e.dma_start(out=d, in_=s), WSB["kv"], pwrites=[WSB["kv"]]))
        pmode["defer"] = True
        off_q = PIECES["q"][0]
        for j in range(4):
            for hh, hd in ((0, j), (1, 4 + j)):
                dst = WS[:, off_q: off_q + 4096].rearrange("p (k c) -> p k c", k=8)[:, :, j * 128 + hh * 64: j * 128 + hh * 64 + 64]
                src = w_in[:, hd * 64:(hd + 1) * 64].rearrange("(k p) c -> p k c", p=128)
                prep(lambda d=dst, s=src: S.dma("pool", lambda e: e.dma_start(out=d, in_=s), WSB["q"], pwrites=[WSB["q"]]))
        colblk("z", w_in, 768, 512)
        wcv("pw", 0, 512, "p (g d) -> p g d", pool_w.rearrange("g c d -> c g d"), g=4)
        wcv("upool", 0, 4096, "p (k c) -> p k c", w_up_pool.rearrange("(k p) c -> p k c", p=128), k=4)
        colblk("ga0", w_in, 1280, 512)
        colblk("ga1", w_in, 1792, 512)
        colblk("gb0", w_in, 2304, 512)
        colblk("gb1", w_in, 2816, 512)
        for j in range(4):
            for (p0, hd) in ((0, 4 + j), (64, j)):
                wcv("uattn", j * 1024, 1024, None, w_up_attn[hd * 64:(hd + 1) * 64, :], prange=(p0, p0 + 64))
        colblk("wo0", w_out, 0, 512)
        colblk("wo1", w_out, 512, 512)
        prep_ffn(2)
        colblk("pg0", w_ple_gate, 0, 512)
        colblk("pg1", w_ple_gate, 512, 512)
        wcv("pp", 0, 2048, "p (k c) -> p k c", w_ple_proj.rearrange("(k p) c -> p k c", p=128), k=2)

        seqA = [f"gu1_{b}" for b in range(11)] + [f"dn1_{i}" for i in range(8)] + ["kv"]
        seqB = (["q", "z", "pw", "ga0", "ga1", "gb0", "gb1", "uattn", "upool", "wo0", "wo1"]
                + [f"gu2_{b}" for b in range(11)] + [f"dn2_{i}" for i in range(8)] + ["pg0", "pg1"])
        wseq = [] if collect else list(wseq_in)
        wstate = {"issued": 0, "next": 0}

        def w_issue_upto(k):
            while wstate["issued"] < min(k, len(wseq)):
                i = wstate["issued"]
                name = wseq[i]
                off, sz = PIECES[name]
                s = i % NSLOT
                S.dma("sp", lambda e, s=s, off=off, sz=sz: e.dma_start(out=wsl[s][:, 0:sz], in_=WS[:, off:off + sz]),
                      WSL[s], reads=[WSB[name]], writes=[WSL[s]])
                wstate["issued"] += 1

        def wget(name, hold=0):
            i = wstate["next"]
            if collect:
                wseq.append(name)
            assert wseq[i] == name, (i, wseq[i], name)
            w_issue_upto(i + NSLOT - hold)
            wstate["next"] += 1
            s = i % NSLOT
            return wsl[s], WSL[s]

        def mm(out_ap, lhsT, rhs, start, stop, reads, wbuf):
            S.op("pe", lambda e: e.matmul(out_ap, lhsT=lhsT, rhs=rhs, start=start, stop=stop), reads=reads, writes=[wbuf])

        def rstd_from_psum(pbank, PBANK, ncols_lo, r, R, col0=0, first=True):
            sl_ = slice(col0, col0 + ncols_lo)
            rd = [PBANK, GP] + ([] if first else [R])
            S.op("act", lambda e: e.activation(out=r[:, sl_], in_=pbank[:, 0:ncols_lo], func=AF.Ln, bias=gcol(GC_EPS)),
                 reads=rd, writes=[R])
            S.op("act", lambda e: e.activation(out=r[:, sl_], in_=r[:, sl_], func=AF.Exp, scale=-0.5), reads=[R], writes=[R])

        def rms_rstd(src_chunks, src_bufs, ncols, ones_ap):
            n = 8
            for c in range(n):
                a_, b_ = src_chunks[c], src_bufs[c]
                sq, SQ = TBFn()
                eng_ = "pool" if c % 2 == 0 else "dve"
                S.op(eng_, lambda e, sq=sq, a=a_: e.tensor_tensor(out=sq[:, 0:ncols], in0=a, in1=a, op=ALU.mult),
                     reads=[b_], writes=[SQ])
                mm(ps[6][:, 0:512], ones_ap, sq[:, 0:512], c == 0, c == n - 1, [SQ, CM], PS[6])
                if ncols > 512:
                    mm(ps[7][:, 0:ncols - 512], ones_ap, sq[:, 512:ncols], c == 0, c == n - 1, [SQ, CM], PS[7])
            r, R = T32n()
            rstd_from_psum(ps[6], PS[6], 512, r, R)
            if ncols > 512:
                rstd_from_psum(ps[7], PS[7], ncols - 512, r, R, col0=512, first=False)
            return r, R

        def norm_to_ubuf(hb, HBb, gc0, ncols, dst=None, DST=None):
            dst = ubuf if dst is None else dst
            DST = UB if DST is None else DST
            r, R = rms_rstd([hb[:, c, 0:ncols] for c in range(8)], HBb, ncols, ONES)
            for c in range(8):
                S.op("dve", lambda e, c=c: e.scalar_tensor_tensor(out=dst[:, c, 0:ncols], in0=hb[:, c, 0:ncols], scalar=gcol(gc0 + c),
                                                                 in1=r[:, 0:ncols], op0=ALU.mult, op1=ALU.mult),
                     reads=[HBb[c], R, GP], writes=[DST[c]])

        def ffn(f, hb, HBb, src=None, SRC=None, side=None, nside=1):
            src = ubuf if src is None else src
            SRC = UB if SRC is None else SRC

            def side_step():
                if side is not None:
                    for _ in range(nside):
                        try:
                            next(side)
                        except StopIteration:
                            pass

            for b in range(11):
                w, Wb = wget(f"gu{f}_{b}")
                wv = w[:, 0:4096].rearrange("p (a k c) -> p a k c", a=2, k=8)
                for jj in range(2):
                    j = 2 * b + jj
                    pg, PG = ps[j % 2], PS[j % 2]
                    pu, PU = ps[2 + j % 2], PS[2 + j % 2]
                    for k in range(8):
                        mm(pg[:], wv[:, 0, k, jj * 128:(jj + 1) * 128], src[:, k, 0:512], k == 0, k == 7, [Wb, SRC[k]], PG)
                    for k in range(8):
                        mm(pu[:], wv[:, 1, k, jj * 128:(jj + 1) * 128], src[:, k, 0:512], k == 0, k == 7, [Wb, SRC[k]], PU)
                    sl, SL = slb[j % 2], SLB[j % 2]
                    S.op("act", lambda e, sl=sl, pg=pg: e.activation(out=sl[:], in_=pg[:], func=AF.Silu), reads=[PG], writes=[SL])
                    S.op("dve", lambda e, sl=sl, pu=pu, j=j: e.tensor_tensor(out=hid[:, j, :], in0=pu[:], in1=sl[:], op=ALU.mult),
                         reads=[PU, SL], writes=[HID[j]])
                    side_step()
            for i in range(8):
                w, Wb = wget(f"dn{f}_{i}")
                wv = w[:, 0:NJ * 128].rearrange("p (j c) -> p j c", j=NJ)
                py, PY = ps[4 + i % 2], PS[4 + i % 2]
                for j in range(NJ):
                    mm(py[:], wv[:, j, :], hid[:, j, :], j == 0, j == NJ - 1, [Wb, HID[j]], PY)
                S.op("dve", lambda e, i=i, py=py: e.scalar_tensor_tensor(out=hb[:, i, 0:512], in0=py[:], scalar=0.5, in1=hb[:, i, 0:512],
                                                                        op0=ALU.mult, op1=ALU.add),
                     reads=[PY, HBb[i]], writes=[HBb[i]])
                side_step()
            if side is not None:
                for _ in side:
                    pass

        def headnorm_rope(psrc, PSRC, gc, tokcol, out_ap, out_buf, pw):
            raw, RAW = T32n()
            S.op("act", lambda e: e.activation(out=raw[:, 0:512], in_=psrc[:], func=AF.Copy), reads=[PSRC], writes=[RAW])
            sq, SQ = TBFn()
            S.op("pool", lambda e: e.tensor_tensor(out=sq[:, 0:512], in0=raw[:, 0:512], in1=raw[:, 0:512], op=ALU.mult), reads=[RAW], writes=[SQ])
            mm(ps[6][:], BONES, sq[:, 0:512], True, True, [SQ, CM], PS[6])
            r, R = T32n()
            rstd_from_psum(ps[6], PS[6], 512, r, R)
            kn, KN = TBFn()
            S.op("dve", lambda e: e.scalar_tensor_tensor(out=kn[:, 0:512], in0=raw[:, 0:512], scalar=gcol(gc), in1=r[:, 0:512],
                                                         op0=ALU.mult, op1=ALU.mult), reads=[RAW, R, GP], writes=[KN])
            mm(ps[7][:], RMAT, kn[:, 0:512], True, True, [KN, CM], PS[7])
            t1, T1 = T32n()
            S.op("dve", lambda e: e.tensor_tensor(out=t1[:, 0:512], in0=kn[:, 0:512], in1=cosb[:], op=ALU.mult), reads=[KN, COS], writes=[T1])
            t2, T2 = T32n()
            S.op("dve", lambda e: e.tensor_tensor(out=t2[:, 0:512], in0=ps[7][:], in1=sinb[:], op=ALU.mult), reads=[PS[7], SIN], writes=[T2])
            if pw:
                S.op("dve", lambda e: e.tensor_tensor(out=out_ap, in0=t1[:, 0:512], in1=t2[:, 0:512], op=ALU.add), reads=[T1, T2], pwrites=[out_buf])
            else:
                S.op("dve", lambda e: e.tensor_tensor(out=out_ap, in0=t1[:, 0:512], in1=t2[:, 0:512], op=ALU.add), reads=[T1, T2], writes=[out_buf])

        def load_rope(t):
            cs = slice(t * NT, (t + 1) * NT)
            S.dma("sp", lambda e: e.dma_start(out=cosb[:], in_=cosT[:, cs]), COS, writes=[COS])
            S.dma("sp", lambda e: e.dma_start(out=sinb[:], in_=sinT[:, cs]), SIN, writes=[SIN])

        xTv = xT.rearrange("(k p) n -> p k n", p=128)
        hSv = hS.rearrange("(k p) n -> p k n", p=128)
        yTv = yT.rearrange("(k p) n -> p k n", p=128)
        pTv = pT.rearrange("(k p) n -> p k n", p=128)

        def load_x(t, bi):
            S.dma("sp", lambda e: e.dma_start(out=hbuf[bi][:, :, 0:512], in_=xTv[:, :, t * NT:(t + 1) * NT]), HB[bi][0], writes=HB[bi])

        UA = [(ubuf, UB), (ubuf2, UB2)]

        def head_gen(t, bi, dve_only=False):
            hb, HBb = hbuf[bi], HB[bi]
            dst, DST = UA[t % 2]
            sqs = []
            for c in range(8):
                sq, SQ = (tbf[c % 4], TBF[c % 4]) if c < 4 else (PT[c % 4], PTB[c % 4])
                eng_ = "pool" if (c % 2 == 0 and not dve_only) else "dve"
                S.op(eng_, lambda e, sq=sq, c=c: e.tensor_tensor(out=sq[:, 0:512], in0=hb[:, c, 0:512], in1=hb[:, c, 0:512], op=ALU.mult),
                     reads=[HBb[c]], writes=[SQ])
                sqs.append((sq, SQ))
                if c % 2 == 1:
                    yield
            yield
            for c in range(8):
                mm(ps[6][:], ONES, sqs[c][0][:, 0:512], c == 0, c == 7, [sqs[c][1], CM], PS[6])
            yield
            yield
            r, R = t32[3], T32[3]
            rstd_from_psum(ps[6], PS[6], 512, r, R)
            yield
            yield
            for c in range(8):
                S.op("dve", lambda e, c=c: e.scalar_tensor_tensor(out=dst[:, c, 0:512], in0=hb[:, c, 0:512], scalar=gcol(GC_F1 + c),
                                                                 in1=r[:, 0:512], op0=ALU.mult, op1=ALU.mult),
                     reads=[HBb[c], R, GP], writes=[DST[c]])
                if c % 4 == 3:
                    yield

        def tail_gen(t, bi, lt, after_u, dve_only=False):
            hb, HBb = hbuf[bi], HB[bi]
            dst, DST = UA[t % 2]
            load_rope(t)
            sqs = []
            for c in range(8):
                sq, SQ = (tbf[c % 4], TBF[c % 4]) if c < 4 else (PT[c % 4], PTB[c % 4])
                eng_ = "pool" if (c % 2 == 0 and not dve_only) else "dve"
                S.op(eng_, lambda e, sq=sq, c=c: e.tensor_tensor(out=sq[:, 0:512], in0=hb[:, c, 0:512], in1=hb[:, c, 0:512], op=ALU.mult),
                     reads=[HBb[c]], writes=[SQ])
                sqs.append((sq, SQ))
                if c % 2 == 1:
                    yield
            yield
            for c in range(8):
                mm(ps[6][:], ONES, sqs[c][0][:, 0:512], c == 0, c == 7, [sqs[c][1], CM], PS[6])
            yield
            yield
            r, R = t32[3], T32[3]
            rstd_from_psum(ps[6], PS[6], 512, r, R)
            yield
            yield
            for c in range(8):
                S.op("dve", lambda e, c=c: e.scalar_tensor_tensor(out=dst[:, c, 0:512], in0=hb[:, c, 0:512], scalar=gcol(GC_MIX + c),
                                                                 in1=r[:, 0:512], op0=ALU.mult, op1=ALU.mult),
                     reads=[HBb[c], R, GP], writes=[DST[c]])
                if c % 4 == 3:
                    yield
            if after_u is not None:
                after_u()
            yield
            w, Wb = wget("kv", hold=1)
            wv = w[:, 0:2048].rearrange("p (k c) -> p k c", k=8)
            for k in range(8):
                mm(ps[7][:], wv[:, k, 0:128], dst[:, k, 0:512], k == 0, k == 7, [Wb, DST[k]], PS[7])
            for s_ in range(4):
                for k in range(8):
                    mm(ps[6][:, s_ * 128:(s_ + 1) * 128], dst[:, k, s_ * 128:(s_ + 1) * 128], wv[:, k, 128:256], k == 0, k == 7, [Wb, DST[k]], PS[6])
            yield
            yield
            pv = ps[6][:].rearrange("p (s c) -> p s c", s=4)
            S.op("act", lambda e: e.activation(out=VA[:, lt * 4:(lt + 1) * 4, 64:192], in_=pv, func=AF.Copy), reads=[PS[6]], pwrites=[VAB])
            raw, RAW = t32[0], T32[0]
            S.op("act", lambda e: e.activation(out=raw[:, 0:512], in_=ps[7][:], func=AF.Copy), reads=[PS[7]], writes=[RAW])
            yield
            sq, SQ = tbf[0], TBF[0]
            S.op("dve" if dve_only else "pool", lambda e: e.tensor_tensor(out=sq[:, 0:512], in0=raw[:, 0:512], in1=raw[:, 0:512], op=ALU.mult), reads=[RAW], writes=[SQ])
            yield
            yield
            mm(ps[6][:], BONES, sq[:, 0:512], True, True, [SQ, CM], PS[6])
            yield
            yield
            r2, R2 = t32[1], T32[1]
            rstd_from_psum(ps[6], PS[6], 512, r2, R2)
            yield
            yield
            kn, KN = tbf[1], TBF[1]
            S.op("dve", lambda e: e.scalar_tensor_tensor(out=kn[:, 0:512], in0=raw[:, 0:512], scalar=gcol(GC_K), in1=r2[:, 0:512],
                                                         op0=ALU.mult, op1=ALU.mult), reads=[RAW, R2, GP], writes=[KN])
            yield
            mm(ps[7][:], RMAT, kn[:, 0:512], True, True, [KN, CM], PS[7])
            S.op("dve", lambda e: e.tensor_tensor(out=raw[:, 0:512], in0=kn[:, 0:512], in1=cosb[:], op=ALU.mult), reads=[KN, COS], writes=[RAW])
            yield
            yield
            S.op("dve", lambda e: e.tensor_tensor(out=r2[:, 0:512], in0=ps[7][:], in1=sinb[:], op=ALU.mult), reads=[PS[7], SIN], writes=[R2])
            S.op("dve", lambda e: e.tensor_tensor(out=KT[:, lt * NT:(lt + 1) * NT], in0=raw[:, 0:512], in1=r2[:, 0:512], op=ALU.add),
                 reads=[RAW, R2], pwrites=[KTB])

        def run_gen(g_):
            if g_ is not None:
                for _ in g_:
                    pass

        def chain2(*gens):
            for g_ in gens:
                if g_ is not None:
                    yield from g_

        def phaseA_run(tiles, n0, blk_base_tile, spills, final_loader):
            nT = len(tiles)
            bis = [(n0 + i) % 2 for i in range(nT)]
            run_gen(head_gen(tiles[0], bis[0], dve_only=(n0 == 0)))
            if nT > 1:
                load_x(tiles[1], bis[1])
            tail = None
            for i, t in enumerate(tiles):
                bi = bis[i]
                hb, HBb = hbuf[bi], HB[bi]
                for _ in range(min(len(late), 6)):
                    late.pop(0)()
                busy_pool = len(late) > 0 or (n0 == 0 and i < 2)
                nxt_head = head_gen(tiles[i + 1], bis[i + 1], dve_only=busy_pool) if i + 1 < nT else None
                src, SRC = UA[t % 2]
                ffn(1, hb, HBb, src=src, SRC=SRC, side=chain2(tail, nxt_head))
                if spills[i]:
                    S.dma("pool", lambda e, t=t, hb=hb: e.dma_start(out=hSv[:, :, t * NT:(t + 1) * NT], in_=hb[:, :, 0:512]), HS[t], reads=HBb, writes=[HS[t]])
                if i + 2 < nT:
                    cb = (lambda tt=tiles[i + 2], bb=bi: load_x(tt, bb))
                else:
                    cb = None
                tail = tail_gen(t, bi, t - blk_base_tile, cb, dve_only=busy_pool)
            run_gen(tail)
            if final_loader is not None:
                final_loader()

        def load_h(t, bi, base, size):
            c0 = t * NT
            lh = base + ((c0 - 8 - base) % size)
            rh = base + ((c0 + NT - base) % size)
            S.dma("sp", lambda e: e.dma_start(out=hbuf[bi][:, :, 0:512], in_=hSv[:, :, c0:c0 + NT]), HB[bi][0],
                  reads=[HS[t]], writes=HB[bi])
            S.dma("sp", lambda e: e.dma_start(out=hbuf[bi][:, :, 512:520], in_=hSv[:, :, lh:lh + 8]), HB[bi][0],
                  reads=[HS[lh // NT]], pwrites=HB[bi])
            S.dma("sp", lambda e: e.dma_start(out=hbuf[bi][:, :, 520:528], in_=hSv[:, :, rh:rh + 8]), HB[bi][0],
                  reads=[HS[rh // NT]], pwrites=HB[bi])

        def norm_side_gen(bi_n):
            hbn, HBn = hbuf[bi_n], HB[bi_n]
            for rnd in range(2):
                sqs = []
                for cc in range(4):
                    c = rnd * 4 + cc
                    sq, SQ = tbf[cc], TBF[cc]
                    eng_ = "pool" if c % 2 == 0 else "dve"
                    S.op(eng_, lambda e, sq=sq, c=c: e.tensor_tensor(out=sq[:, 0:528], in0=hbn[:, c, 0:528], in1=hbn[:, c, 0:528], op=ALU.mult),
                         reads=[HBn[c]], writes=[SQ])
                    sqs.append((sq, SQ))
                    if cc % 2 == 1:
                        yield
                yield
                for cc in range(4):
                    c = rnd * 4 + cc
                    mm(ps[6][:, 0:512], ONES, sqs[cc][0][:, 0:512], c == 0, c == 7, [sqs[cc][1], CM], PS[6])
                    mm(ps[7][:, 0:16], ONES, sqs[cc][0][:, 512:528], c == 0, c == 7, [sqs[cc][1], CM], PS[7])
                yield
                yield
            r, R = t32[3], T32[3]
            rstd_from_psum(ps[6], PS[6], 512, r, R)
            rstd_from_psum(ps[7], PS[7], 16, r, R, col0=512, first=False)
            yield
            yield
            for c in range(8):
                S.op("dve", lambda e, c=c: e.scalar_tensor_tensor(out=ubuf2[:, c, 0:528], in0=hbn[:, c, 0:528], scalar=gcol(GC_MIX + c),
                                                                 in1=r[:, 0:528], op0=ALU.mult, op1=ALU.mult),
                     reads=[HBn[c], R, GP], writes=[UB2[c]])
                if c % 4 == 3:
                    yield

        def zpool_ops(g, first, last, flcol, pz, PZ, ph, PH):
            S.op("act", lambda e: e.activation(out=zext[:, g, 8:520], in_=pz[:], func=AF.Copy), reads=[PZ], pwrites=[ZX[g]])
            if first:
                S.op("dve", lambda e: e.tensor_scalar(out=zext[:, g, 0:8], in0=ph[:, 0:8], scalar1=gcol(flcol), scalar2=None, op0=ALU.mult), reads=[PH, GP], pwrites=[ZX[g]])
            else:
                S.op("dve", lambda e: e.tensor_copy(out=zext[:, g, 0:8], in_=ph[:, 0:8]), reads=[PH], pwrites=[ZX[g]])
            if last:
                S.op("dve", lambda e: e.tensor_scalar(out=zext[:, g, 520:528], in0=ph[:, 8:16], scalar1=gcol(flcol + 1), scalar2=None, op0=ALU.mult), reads=[PH, GP], pwrites=[ZX[g]])
            else:
                S.op("dve", lambda e: e.tensor_copy(out=zext[:, g, 520:528], in_=ph[:, 8:16]), reads=[PH], pwrites=[ZX[g]])

        def pool_chain(g):
            z = zext[:, g, :]
            add = lambda o, a, b_, rd, wr: S.op("pool", lambda e: e.tensor_tensor(out=o, in0=a, in1=b_, op=ALU.add), reads=rd, writes=wr)
            add(tmpA[:, 1:528], z[:, 0:527], z[:, 1:528], [ZX[g]], [TA])
            cur, CUR, oth, OTH = tmpA, TA, tmpB, TB
            lo, hi, sh = 1, 528, 1
            for lvl in range(g):
                nlo, nhi = lo + sh, hi - sh
                add(oth[:, nlo:nhi], cur[:, nlo - sh:nhi - sh], cur[:, nlo + sh:nhi + sh], [CUR], [OTH])
                cur, CUR, oth, OTH = oth, OTH, cur, CUR
                lo, hi, sh = nlo, nhi, sh * 2
            S.op("pool", lambda e, cur=cur, oth=oth: e.tensor_tensor(out=oth[:, 8:520], in0=cur[:, 8:520], in1=invcb[:, g, :], op=ALU.mult),
                 reads=[CUR, INVC], writes=[OTH])
            S.op("pool", lambda e, oth=oth: e.tensor_tensor(out=pooled[:, g, :], in0=oth[:, 8:520], in1=zext[:, g, 8:520], op=ALU.subtract),
                 reads=[OTH, ZX[g]], writes=[PO[g]])

        def q_side_gen(t_n):
            load_rope(t_n)
            offq = PIECES["q"][0]
            zflat = zext[:].rearrange("p g c -> p (g c)").bitcast(BF16)[:, 0:4096]
            S.dma("sp", lambda e: e.dma_start(out=zflat, in_=WS[:, offq:offq + 4096]), ZX[0], reads=[WSB["q"]], writes=ZX)
            wq = zflat.rearrange("p (k c) -> p k c", k=8)
            yield
            raw, RAW = t32[0], T32[0]
            r2, R2 = t32[1], T32[1]
            sq, SQ = tbf[0], TBF[0]
            kn, KN = tbf[1], TBF[1]
            for j in range(4):
                for k in range(8):
                    mm(ps[7][:], wq[:, k, j * 128:(j + 1) * 128], ubuf2[:, k, 0:512], k == 0, k == 7, ZX + [UB2[k]], PS[7])
                yield
                yield
                yield
                S.op("act", lambda e: e.activation(out=raw[:, 0:512], in_=ps[7][:], func=AF.Copy), reads=[PS[7]], writes=[RAW])
                yield
                yield
                S.op("pool", lambda e: e.tensor_tensor(out=sq[:, 0:512], in0=raw[:, 0:512], in1=raw[:, 0:512], op=ALU.mult), reads=[RAW], writes=[SQ])
                yield
                yield
                yield
                mm(ps[6][:], BONES, sq[:, 0:512], True, True, [SQ, CM], PS[6])
                yield
                yield
                yield
                rstd_from_psum(ps[6], PS[6], 512, r2, R2)
                yield
                yield
                yield
                S.op("dve", lambda e: e.scalar_tensor_tensor(out=kn[:, 0:512], in0=raw[:, 0:512], scalar=gcol(GC_Q), in1=r2[:, 0:512],
                                                             op0=ALU.mult, op1=ALU.mult), reads=[RAW, R2, GP], writes=[KN])
                yield
                yield
                mm(ps[7][:], RMAT, kn[:, 0:512], True, True, [KN, CM], PS[7])
                S.op("dve", lambda e: e.tensor_tensor(out=raw[:, 0:512], in0=kn[:, 0:512], in1=cosb[:], op=ALU.mult), reads=[KN, COS], writes=[RAW])
                yield
                yield
                yield
                S.op("dve", lambda e: e.tensor_tensor(out=r2[:, 0:512], in0=ps[7][:], in1=sinb[:], op=ALU.mult), reads=[PS[7], SIN], writes=[R2])
                S.op("dve", lambda e, j=j: e.tensor_tensor(out=qT[:, j, :], in0=raw[:, 0:512], in1=r2[:, 0:512], op=ALU.add), reads=[RAW, R2], writes=[QT[j]])
                yield

        def zpool_side_gen(oi_n, first_n, last_n, flcol_n):
            ocs_n = slice(oi_n * NT, (oi_n + 1) * NT)
            S.dma("sp", lambda e: e.dma_start(out=invcb[:], in_=invcT[:, ocs_n].partition_broadcast(128)), INVC, writes=[INVC])
            offz = PIECES["z"][0]
            S.dma("sp", lambda e: e.dma_start(out=attnT[:], in_=WS[:, offz:offz + 2048].rearrange("p (k c) -> p k c", k=4)), AT[0],
                  reads=[WSB["z"]], writes=AT)
            for kk in range(4):
                S.dma("sp", lambda e, kk=kk: e.dma_start(out=PT[kk][:], in_=WS[:, offz + 2048 + kk * 512: offz + 2048 + (kk + 1) * 512]), PTB[kk],
                      reads=[WSB["z"]], writes=[PTB[kk]])
            offw = PIECES["pw"][0]
            S.dma("sp", lambda e: e.dma_start(out=pwv, in_=WS[:, offw:offw + 512]), RSTDE, reads=[WSB["pw"]], writes=[RSTDE])
            wpw = pwv.rearrange("p (g d) -> p g d", g=4)
            yield

            def zw(kc, g):
                if kc < 4:
                    return attnT[:, kc, g * 128:(g + 1) * 128], AT[kc]
                return PT[kc - 4][:, g * 128:(g + 1) * 128], PTB[kc - 4]

            for g in range(4):
                for k in range(8):
                    wap, WB_ = zw(k, g)
                    mm(ps[7][:], wap, ubuf2[:, k, 0:512], k == 0, k == 7, [WB_, UB2[k]], PS[7])
                for k in range(8):
                    wap, WB_ = zw(k, g)
                    mm(ps[6][:, 0:16], wap, ubuf2[:, k, 512:528], k == 0, k == 7, [WB_, UB2[k]], PS[6])
                yield
                yield
                yield
                zpool_ops(g, first_n, last_n, flcol_n, ps[7], PS[7], ps[6], PS[6])
                yield
                yield
                pool_chain(g)
                for _ in range(4 + g):
                    yield
                mm(ps[7][:], wpw[:, g, :], pooled[:, g, :], True, True, [RSTDE, PO[g]], PS[7])
                yield
                yield
                yield
                S.op("act", lambda e, g=g: e.activation(out=pooled[:, g, :], in_=ps[7][:], func=AF.Copy, scale=gcol(GC_PS + g)),
                     reads=[PS[7], GP], writes=[PO[g]])
                yield
                yield

        def phaseB(t, bi, oi, base, size, nkc, first, last, flcol, nxt, side_prev, defer_ple, pre_normed=False, norm_next=False, next_info=None):
            hb, HBb = hbuf[bi], HB[bi]
            while late:
                late.pop(0)()
            if not pre_normed:
                load_rope(t)
            ocs = slice(oi * NT, (oi + 1) * NT)
            if not pre_normed:
                S.dma("sp", lambda e: e.dma_start(out=invcb[:], in_=invcT[:, ocs].partition_broadcast(128)), INVC, writes=[INVC])
            def store_y():
                S.dma("pool", lambda e: e.dma_start(out=yTv[:, :, ocs], in_=hb[:, :, 0:512]), YS[bi], reads=HBb, pwrites=[YS[bi]], is_out=True)
            if STAGE == 1:
                store_y()
            if pre_normed:
                U_, UB_ = ubuf2, UB2
            else:
                U_, UB_ = ubuf, UB
                norm_to_ubuf(hb, HBb, GC_MIX, 528)
            if not pre_normed:
                w, Wb = wget("q")
                wv = w[:, 0:4096].rearrange("p (k c) -> p k c", k=8)
                for j in range(4):
                    pq, PQ = PSRn()
                    for k in range(8):
                        mm(pq[:], wv[:, k, j * 128:(j + 1) * 128], U_[:, k, 0:512], k == 0, k == 7, [Wb, UB_[k]], PQ)
                    headnorm_rope(pq, PQ, GC_Q, None, qT[:, j, :], QT[j], False)
            if STAGE == 30 and oi == 0:
                pass
            if not pre_normed:
                w, Wb = wget("z")
                wv = w[:, 0:4096].rearrange("p (k c) -> p k c", k=8)
                for g in range(4):
                    pz, PZ = PSRn()
                    for k in range(8):
                        mm(pz[:], wv[:, k, g * 128:(g + 1) * 128], U_[:, k, 0:512], k == 0, k == 7, [Wb, UB_[k]], PZ)
                    for k in range(8):
                        mm(ps[4][:, g * 16:(g + 1) * 16], wv[:, k, g * 128:(g + 1) * 128], U_[:, k, 512:528], k == 0, k == 7, [Wb, UB_[k]], PS[4])
                    S.op("act", lambda e, g=g, pz=pz: e.activation(out=zext[:, g, 8:520], in_=pz[:], func=AF.Copy), reads=[PZ], pwrites=[ZX[g]])
                    if first:
                        S.op("dve", lambda e, g=g: e.tensor_scalar(out=zext[:, g, 0:8], in0=ps[4][:, g * 16:g * 16 + 8], scalar1=gcol(flcol), scalar2=None, op0=ALU.mult), reads=[PS[4], GP], pwrites=[ZX[g]])
                    else:
                        S.op("dve", lambda e, g=g: e.tensor_copy(out=zext[:, g, 0:8], in_=ps[4][:, g * 16:g * 16 + 8]), reads=[PS[4]], pwrites=[ZX[g]])
                    if last:
                        S.op("dve", lambda e, g=g: e.tensor_scalar(out=zext[:, g, 520:528], in0=ps[4][:, g * 16 + 8:g * 16 + 16], scalar1=gcol(flcol + 1),
                                                                   scalar2=None, op0=ALU.mult), reads=[PS[4], GP], pwrites=[ZX[g]])
                    else:
                        S.op("dve", lambda e, g=g: e.tensor_copy(out=zext[:, g, 520:528], in_=ps[4][:, g * 16 + 8:g * 16 + 16]), reads=[PS[4]], pwrites=[ZX[g]])
                w, Wb = wget("pw")
                wpw = w[:, 0:512].rearrange("p (g d) -> p g d", g=4)
                for g in range(4):
                    z = zext[:, g, :]
                    add = lambda o, a, b_, rd, wr: S.op("pool", lambda e: e.tensor_tensor(out=o, in0=a, in1=b_, op=ALU.add), reads=rd, writes=wr)
                    add(tmpA[:, 1:528], z[:, 0:527], z[:, 1:528], [ZX[g]], [TA])
                    cur, CUR, oth, OTH = tmpA, TA, tmpB, TB
                    lo, hi, sh = 1, 528, 1
                    for lvl in range(g):
                        nlo, nhi = lo + sh, hi - sh
                        add(oth[:, nlo:nhi], cur[:, nlo - sh:nhi - sh], cur[:, nlo + sh:nhi + sh], [CUR], [OTH])
                        cur, CUR, oth, OTH = oth, OTH, cur, CUR
                        lo, hi, sh = nlo, nhi, sh * 2
                    S.op("pool", lambda e, cur=cur, oth=oth, g=g: e.tensor_tensor(out=oth[:, 8:520], in0=cur[:, 8:520], in1=invcb[:, g, :], op=ALU.mult),
                         reads=[CUR, INVC], writes=[OTH])
                    S.op("pool", lambda e, oth=oth, g=g: e.tensor_tensor(out=pooled[:, g, :], in0=oth[:, 8:520], in1=zext[:, g, 8:520], op=ALU.subtract),
                         reads=[OTH, ZX[g]], writes=[PO[g]])
                    pm, PM = PSRn()
                    mm(pm[:], wpw[:, g, :], pooled[:, g, :], True, True, [Wb, PO[g]], PM)
                    S.op("act", lambda e, g=g, pm=pm: e.activation(out=pooled[:, g, :], in_=pm[:], func=AF.Copy, scale=gcol(GC_PS + g)),
                         reads=[PM, GP], writes=[PO[g]])
            def gates_gen():
                for gi, nm in enumerate(("ga0", "ga1", "gb0", "gb1")):
                    w, Wb = wget(nm)
                    wv = w[:, 0:4096].rearrange("p (k c) -> p k c", k=8)
                    for oc in range(4):
                        idx = gi * 4 + oc
                        for k in range(8):
                            mm(ps[7][:], wv[:, k, oc * 128:(oc + 1) * 128], U_[:, k, 0:512], k == 0, k == 7, [Wb, UB_[k]], PS[7])
                        yield
                        yield
                        yield
                        sg, SG = T32n()
                        S.op("act", lambda e, sg=sg: e.activation(out=sg[:, 0:512], in_=ps[7][:], func=AF.Exp, scale=-1.0), reads=[PS[7]], writes=[SG])
                        yield
                        yield
                        S.op("dve", lambda e, sg=sg: e.tensor_scalar_add(out=sg[:, 0:512], in0=sg[:, 0:512], scalar1=1.0), reads=[SG], writes=[SG])
                        S.op("dve", lambda e, sg=sg, idx=idx: e.reciprocal(out=hid[:, idx, :], in_=sg[:, 0:512]), reads=[SG], writes=[HID[idx]])
                        yield

            def chain_gens(*gens):
                for g_ in gens:
                    if g_ is not None:
                        yield from g_

            side = chain_gens(gates_gen(), side_prev)

            def side_step():
                for _ in range(1 if nkc >= 64 else 2):
                    try:
                        next(side)
                    except StopIteration:
                        pass

            plo, PLO, phi, PHI = ps[4], PS[4], ps[5], PS[5]

            def issue_S(j, kc):
                ks = slice(kc * 128, (kc + 1) * 128)
                s_lo, SLO = PSRn()
                mm(s_lo[:], KT[0:64, ks], qT[0:64, j, :], True, True, [KTB, QT[j]], SLO)
                s_hi, SHI = PSRn()
                mm(s_hi[:], KT[64:128, ks], qT[64:128, j, :], True, True, [KTB, QT[j]], SHI)
                return s_lo, SLO, s_hi, SHI

            def epilogue1(j, c_lo, CLO, c_hi, CHI):
                S.op("dve", lambda e: e.reciprocal(out=attnT[0:64, j, :], in_=c_lo[0:64, 0:512]), reads=[CLO], writes=[AT[j]])
                S.op("dve", lambda e: e.reciprocal(out=attnT[64:128, j, :], in_=c_hi[64:128, 0:512]), reads=[CHI], pwrites=[AT[j]])

            def epilogue2(j, c_lo, CLO, c_hi, CHI):
                psw, PSW = PSRn()
                mm(psw[:], SWAPM, attnT[:, j, :], True, True, [AT[j], CM], PSW)
                S.op("dve", lambda e: e.tensor_tensor(out=attnT[0:64, j, :], in0=psw[0:64, :], in1=c_hi[0:64, 0:512], op=ALU.mult),
                     reads=[PSW, CHI], writes=[AT[j]])
                S.op("dve", lambda e: e.tensor_tensor(out=attnT[64:128, j, :], in0=psw[64:128, :], in1=c_lo[64:128, 0:512], op=ALU.mult),
                     reads=[PSW, CLO], pwrites=[AT[j]])

            pend_epi = None
            pend_first = None
            for j in range(4):
                pend = pend_first if pend_first is not None else issue_S(j, 0)
                pend_first = None
                for kc in range(nkc):
                    s_lo, SLO, s_hi, SHI = pend
                    if kc + 1 < nkc:
                        pend = issue_S(j, kc + 1)
                    elif j + 1 < 4:
                        pend_first = issue_S(j + 1, 0)
                    p_lo, PLb = PTn()
                    S.op("act", lambda e, p_lo=p_lo, s_lo=s_lo: e.activation(out=p_lo[:], in_=s_lo[:], func=AF.Exp, scale=0.125), reads=[SLO], writes=[PLb])
                    p_hi, PHb = PTn()
                    S.op("act", lambda e, p_hi=p_hi, s_hi=s_hi: e.activation(out=p_hi[:], in_=s_hi[:], func=AF.Exp, scale=0.125), reads=[SHI], writes=[PHb])
                    mm(plo[:], VA[:, kc, 0:128], p_lo[:], kc == 0, kc == nkc - 1, [VAB, PLb], PLO)
                    mm(phi[:], VA[:, kc, 128:256], p_hi[:], kc == 0, kc == nkc - 1, [VAB, PHb], PHI)
                    if kc == 2 and pend_epi is not None:
                        epilogue1(*pend_epi)
                    if kc == 14 and pend_epi is not None:
                        epilogue2(*pend_epi)
                        pend_epi = None
                    side_step()
                c_lo, CLO = tmpA, TA
                S.op("act", lambda e, c_lo=c_lo: e.activation(out=c_lo[:, 0:512], in_=plo[:], func=AF.Copy), reads=[PLO], writes=[CLO])
                c_hi, CHI = tmpB, TB
                S.op("act", lambda e, c_hi=c_hi: e.activation(out=c_hi[:, 0:512], in_=phi[:], func=AF.Copy), reads=[PHI], writes=[CHI])
                pend_epi = (j, c_lo, CLO, c_hi, CHI)
            epilogue1(*pend_epi)
            for _ in side:
                pass
            epilogue2(*pend_epi)
            if nxt is not None:
                nxt()
            if STAGE == 30 and oi == 0:
                dbg_put(6, attnT[:, 0, :], AT[0])
                dbg_put(0, attnT[:, 1, :], AT[1])
                dbg_put(1, attnT[:, 2, :], AT[2])
                dbg_put(2, attnT[:, 3, :], AT[3])
            wa, WAb = wget("uattn")
            wav = wa[:, 0:4096].rearrange("p (j c) -> p j c", j=4)
            wp, WPb = wget("upool", hold=1)
            wpv = wp[:, 0:4096].rearrange("p (j c) -> p j c", j=4)
            for c in range(8):
                pa, PA = ps[c % 2], PS[c % 2]
                pbb, PBB = ps[2 + c % 2], PS[2 + c % 2]
                for j in range(4):
                    mm(pa[:], wav[:, j, c * 128:(c + 1) * 128], attnT[:, j, :], j == 0, j == 3, [WAb, AT[j]], PA)
                for j in range(4):
                    mm(pbb[:], wpv[:, j, c * 128:(c + 1) * 128], pooled[:, j, :], j == 0, j == 3, [WPb, PO[j]], PBB)
                t1, T1 = T32n()
                S.op("dve", lambda e, t1=t1, pa=pa, c=c: e.tensor_tensor(out=t1[:, 0:512], in0=pa[:], in1=hid[:, c, :], op=ALU.mult), reads=[PA, HID[c]], writes=[T1])
                if STAGE == 30 and oi == 0 and c == 0:
                    dbg_put(3, pa[:], PA)
                    dbg_put(4, hid[:, 0, :], HID[0])
                    dbg_put(5, t1[:, 0:512], T1)
                    dbg_flush()
                t2, T2 = T32n()
                S.op("dve", lambda e, t2=t2, pbb=pbb, c=c: e.tensor_tensor(out=t2[:, 0:512], in0=pbb[:], in1=hid[:, 8 + c, :], op=ALU.mult), reads=[PBB, HID[8 + c]], writes=[T2])
                if STAGE == 21:
                    S.op("pool", lambda e, t1=t1, t2=t2, c=c: e.tensor_copy(out=ubuf[:, c, 0:512], in_=t1[:, 0:512]), reads=[T1, T2], writes=[UB[c]])
                elif STAGE == 22:
                    S.op("pool", lambda e, t1=t1, t2=t2, c=c: e.tensor_copy(out=ubuf[:, c, 0:512], in_=t2[:, 0:512]), reads=[T1, T2], writes=[UB[c]])
                else:
                    S.op("pool", lambda e, t1=t1, t2=t2, c=c: e.tensor_tensor(out=ubuf[:, c, 0:512], in0=t1[:, 0:512], in1=t2[:, 0:512], op=ALU.add), reads=[T1, T2], writes=[UB[c]])
            prev_sq = None
            for half in range(2):
                w, Wb = wget(f"wo{half}")
                wv = w[:, 0:4096].rearrange("p (k c) -> p k c", k=8)
                for ii in range(4):
                    i = half * 4 + ii
                    py, PY = ps[4 + i % 2], PS[4 + i % 2]
                    for k in range(8):
                        mm(py[:], wv[:, k, ii * 128:(ii + 1) * 128], ubuf[:, k, 0:512], k == 0, k == 7, [Wb, UB[k]], PY)
                    if prev_sq is not None:
                        mm(ps[6][:], ONES, prev_sq[0][:, 0:512], i == 1, False, [prev_sq[1], CM], PS[6])
                    S.op("dve", lambda e, i=i, py=py: e.tensor_tensor(out=hb[:, i, 0:512], in0=py[:], in1=hb[:, i, 0:512], op=ALU.add), reads=[PY, HBb[i]], writes=[HBb[i]])
                    sq, SQ = TBFn()
                    S.op("pool" if i % 2 == 0 else "dve", lambda e, sq=sq, i=i: e.tensor_tensor(out=sq[:, 0:512], in0=hb[:, i, 0:512], in1=hb[:, i, 0:512], op=ALU.mult),
                         reads=[HBb[i]], writes=[SQ])
                    prev_sq = (sq, SQ)
            mm(ps[6][:], ONES, prev_sq[0][:, 0:512], False, True, [prev_sq[1], CM], PS[6])
            if STAGE in (2, 21, 22):
                store_y()
            if STAGE == 0:
                rf, RF = T32n()
                rstd_from_psum(ps[6], PS[6], 512, rf, RF)
                for c in range(8):
                    S.op("dve", lambda e, c=c: e.scalar_tensor_tensor(out=ubuf[:, c, 0:512], in0=hb[:, c, 0:512], scalar=gcol(GC_F2 + c),
                                                                     in1=rf[:, 0:512], op0=ALU.mult, op1=ALU.mult),
                         reads=[HBb[c], RF, GP], writes=[UB[c]])
            else:
                norm_to_ubuf(hb, HBb, GC_F2, 512)
            if norm_next:
                side2 = chain2(norm_side_gen(1 - bi), q_side_gen(next_info[4]), zpool_side_gen(*next_info[0:4]))
            else:
                side2 = None
            ffn(2, hb, HBb, side=side2, nside=5)
            if STAGE == 3:
                store_y()
            def ple_gen():
                if not ppstate["loaded"]:
                    offp, szp = PIECES["pp"]
                    S.dma("sp", lambda e: e.dma_start(out=ppb[:], in_=WS[:, offp:offp + szp]), PPB, reads=[WSB["pp"]], writes=[PPB])
                    ppstate["loaded"] = True
                S.dma("pool", lambda e: e.dma_start(out=pb[:], in_=pTv[:, :, ocs]), PB, writes=[PB])
                wvp = ppb[:].rearrange("p (k c) -> p k c", k=2)
                yield

                def eproj(i):
                    for k in range(2):
                        mm(ps[7][:], wvp[:, k, i * 128:(i + 1) * 128], pb[:, k, :], k == 0, k == 1, [PPB, PB], PS[7])

                prev = None
                for i in range(8):
                    eproj(i)
                    if prev is not None:
                        mm(ps[6][:], ONES, prev[0][:, 0:512], i == 1, False, [prev[1], CM], PS[6])
                    yield
                    yield
                    yield
                    e32, E32 = T32n()
                    S.op("dve", lambda e, e32=e32: e.tensor_copy(out=e32[:, 0:512], in_=ps[7][:]), reads=[PS[7]], writes=[E32])
                    sq, SQ = TBFn()
                    S.op("pool", lambda e, sq=sq, e32=e32: e.tensor_tensor(out=sq[:, 0:512], in0=e32[:, 0:512], in1=e32[:, 0:512], op=ALU.mult),
                         reads=[E32], writes=[SQ])
                    prev = (sq, SQ)
                    yield
                    yield
                    yield
                mm(ps[6][:], ONES, prev[0][:, 0:512], False, True, [prev[1], CM], PS[6])
                yield
                yield
                yield
                rstd_from_psum(ps[6], PS[6], 512, rstde, RSTDE)
                yield
                prev = None
                for c in range(8):
                    sq, SQ = TBFn()
                    S.op("pool", lambda e, sq=sq, c=c: e.tensor_tensor(out=sq[:, 0:512], in0=hb[:, c, 0:512], in1=hb[:, c, 0:512], op=ALU.mult),
                         reads=[HBb[c]], writes=[SQ])
                    if prev is not None:
                        mm(ps[6][:], ONES, prev[0][:, 0:512], c == 1, False, [prev[1], CM], PS[6])
                    prev = (sq, SQ)
                    yield
                    yield
                    yield
                mm(ps[6][:], ONES, prev[0][:, 0:512], False, True, [prev[1], CM], PS[6])
                yield
                yield
                yield
                r3, R3 = T32n()
                rstd_from_psum(ps[6], PS[6], 512, r3, R3)
                yield
                yield
                for c in range(8):
                    S.op("dve", lambda e, c=c: e.scalar_tensor_tensor(out=ubuf2[:, c, 0:512], in0=hb[:, c, 0:512], scalar=gcol(GC_PG + c),
                                                                     in1=r3[:, 0:512], op0=ALU.mult, op1=ALU.mult),
                         reads=[HBb[c], R3, GP], writes=[UB2[c]])
                yield
                for half in range(2):
                    w, Wb = wget(f"pg{half}")
                    wv = w[:, 0:4096].rearrange("p (k c) -> p k c", k=8)
                    for ii in range(4):
                        i = half * 4 + ii
                        for k in range(8):
                            mm(ps[7][:], wv[:, k, ii * 128:(ii + 1) * 128], ubuf2[:, k, 0:512], k == 0, k == 7, [Wb, UB2[k]], PS[7])
                        yield
                        yield
                        yield
                        gt, GT = T32n()
                        S.op("act", lambda e, gt=gt: e.activation(out=gt[:, 0:512], in_=ps[7][:], func=AF.Exp, scale=-1.0), reads=[PS[7]], writes=[GT])
                        yield
                        yield
                        S.op("dve", lambda e, gt=gt: e.tensor_scalar_add(out=gt[:, 0:512], in0=gt[:, 0:512], scalar1=1.0), reads=[GT], writes=[GT])
                        S.op("dve", lambda e, gt=gt: e.reciprocal(out=gt[:, 0:512], in_=gt[:, 0:512]), reads=[GT], writes=[GT])
                        eproj(i)
                        yield
                        yield
                        yield
                        t1, T1 = T32n()
                        S.op("dve", lambda e, t1=t1, i=i: e.scalar_tensor_tensor(out=t1[:, 0:512], in0=ps[7][:], scalar=gcol(GC_PP + i), in1=rstde[:, 0:512],
                                                                                op0=ALU.mult, op1=ALU.mult), reads=[PS[7], RSTDE, GP], writes=[T1])
                        S.op("pool", lambda e, t1=t1, gt=gt: e.tensor_tensor(out=t1[:, 0:512], in0=t1[:, 0:512], in1=gt[:, 0:512], op=ALU.mult), reads=[T1, GT], writes=[T1])
                        S.op("dve", lambda e, t1=t1, i=i: e.tensor_tensor(out=hb[:, i, 0:512], in0=hb[:, i, 0:512], in1=t1[:, 0:512], op=ALU.add), reads=[T1, HBb[i]], writes=[HBb[i]])
                        yield
                if STAGE == 0:
                    store_y()

            g_ = ple_gen()
            if defer_ple:
                return g_
            for _ in g_:
                pass
            return None

        plan = []
        for t in range(16):
            plan.append(("A", t, 0, t < 8 or t in (8, 15)))
        for oi in range(8):
            plan.append(("B", oi, oi, 0, TP, 64, oi == 0, oi == 7, GC_FL))
        for t in range(16, 24):
            plan.append(("A", t, 16, t < 20 or t in (20, 23)))
        for oi in range(8, 12):
            plan.append(("B", 8 + oi, oi, TP, TS, 32, oi == 8, oi == 11, GC_FL + 2))

        def mk_loader(step, bi):
            if step[0] == "A":
                return lambda: load_x(step[1], bi)
            return lambda: load_h(step[1], bi, step[3], step[4])

        mk_loader(plan[0], 0)()
        side_prev = None
        n = 0
        while n < len(plan):
            step = plan[n]
            if step[0] == "A":
                m = n
                while m < len(plan) and plan[m][0] == "A":
                    m += 1
                tiles = [plan[q][1] for q in range(n, m)]
                spills = [plan[q][3] for q in range(n, m)]
                fl = mk_loader(plan[m], m % 2) if m < len(plan) else None
                phaseA_run(tiles, n, step[2], spills, fl)
                n = m
            else:
                bi = n % 2
                nxt = mk_loader(plan[n + 1], 1 - bi) if n + 1 < len(plan) else None
                defer = n + 1 < len(plan) and plan[n + 1][0] == "B"
                pre = n > 0 and plan[n - 1][0] == "B"
                side_prev = phaseB(step[1], bi, step[2], step[3], step[4], step[5], step[6], step[7], step[8], nxt, side_prev, defer,
                                   pre_normed=pre, norm_next=defer,
                                   next_info=((plan[n + 1][2], plan[n + 1][6], plan[n + 1][7], plan[n + 1][8], plan[n + 1][1]) if defer else None))
                n += 1
        if collect:
            return list(wseq)
        assert wstate["next"] == len(wseq), (wstate, len(wseq))
        with nc.allow_low_precision(reason="bf16 matmul operands (reference tolerance is bf16-level)"):
            S.emit()
    return nc


def _rope_tables(T, shift):
    t = (np.arange(T) + shift) % T
    row = (t // 64).astype(np.float64)
    col = (t % 64).astype(np.float64)
    inv = 10000.0 ** (-np.arange(0, 32, 2, dtype=np.float64) / 32.0)
    p = np.arange(128)
    d = p % 64
    axis = d // 32
    j = d % 16
    pos = np.where(axis[:, None] == 0, row[None, :], col[None, :])
    ang = pos.astype(np.float32) * inv.astype(np.float32)[j][:, None]
    return np.cos(ang).astype(np.float32), np.sin(ang).astype(np.float32)


def _invc(T, t0, n):
    t = np.arange(t0, t0 + n)
    out = np.zeros((4, n), np.float32)
    for g, w in enumerate((2, 4, 8, 16)):
        left = w // 2
        right = w - 1 - left
        lo = np.maximum(t - left, 0)
        hi = np.minimum(t + right, T - 1)
        out[g] = 1.0 / (hi - lo + 1).astype(np.float32)
    return out


def _cpack():
    c = np.zeros((128, 512), np.float32)
    c[:, 0:128] = 1.0 / 1024.0
    c[0:64, 128:192] = 1.0 / 64.0
    c[64:128, 192:256] = 1.0 / 64.0
    for m in range(128):
        if (m % 32) < 16:
            c[m + 16, 256 + m] = -1.0
        else:
            c[m - 16, 256 + m] = 1.0
    for m in range(128):
        c[(m + 64) % 128, 384 + m] = 1.0
    return c


_NC_CACHE = {}


def kernel(x_prompt, x_sample, p_prompt, p_sample, ffn1_norm, ffn1_w_gate, ffn1_w_up, ffn1_w_down,
           mix_norm, w_in, q_norm, k_norm, w_up_attn, pool_w, pool_scale, w_up_pool, w_out,
           ffn2_norm, ffn2_w_gate, ffn2_w_up, ffn2_w_down, ple_gate_norm, w_ple_gate, w_ple_proj,
           ple_post_norm):
    f32 = lambda a: np.ascontiguousarray(np.asarray(a, dtype=np.float32))
    x_prompt, x_sample, p_prompt, p_sample = map(f32, (x_prompt, x_sample, p_prompt, p_sample))
    if "nc" not in _NC_CACHE:
        _NC_CACHE["nc"] = build_nc(build_nc(None))
    nc = _NC_CACHE["nc"]
    col8 = lambda g: f32(g).reshape(8, 128).T
    shared = {
        "ffn1_w_gate": f32(ffn1_w_gate[0]), "ffn1_w_up": f32(ffn1_w_up[0]), "ffn1_w_down": f32(ffn1_w_down[0]),
        "ffn2_w_gate": f32(ffn2_w_gate[0]), "ffn2_w_up": f32(ffn2_w_up[0]), "ffn2_w_down": f32(ffn2_w_down[0]),
        "w_in": f32(w_in[0]), "w_up_attn": f32(w_up_attn[0]), "pool_w": f32(pool_w[0]), "w_up_pool": f32(w_up_pool[0]),
        "w_out": f32(w_out[0]), "w_ple_gate": f32(w_ple_gate[0]), "w_ple_proj": f32(w_ple_proj[0]),
        "cpack": _cpack(),
    }
    gbase = np.zeros((128, GC_N), np.float32)
    gbase[:, GC_F1:GC_F1 + 8] = col8(ffn1_norm[0])
    gbase[:, GC_MIX:GC_MIX + 8] = col8(mix_norm[0])
    gbase[:, GC_F2:GC_F2 + 8] = col8(ffn2_norm[0])
    gbase[:, GC_PG:GC_PG + 8] = col8(ple_gate_norm[0])
    gbase[:, GC_PP:GC_PP + 8] = col8(ple_post_norm[0])
    gbase[:, GC_PS:GC_PS + 4] = f32(pool_scale[0]).reshape(4, 128).T
    gbase[:, GC_Q] = np.tile(f32(q_norm[0]), 2)
    gbase[:, GC_K] = np.tile(f32(k_norm[0]), 2)
    gbase[:, GC_EPS] = EPS
    in_maps = []
    for c in range(8):
        i, half = c // 2, c % 2
        sp, ss = half * (TP // 2), half * (TS // 2)
        xp = np.roll(x_prompt[i], -sp, axis=0)
        xs = np.roll(x_sample[i], -ss, axis=0)
        xT = np.ascontiguousarray(np.concatenate([xp, xs], axis=0).T)
        pT = np.ascontiguousarray(np.concatenate([p_prompt[0, i, sp:sp + TP // 2], p_sample[0, i, ss:ss + TS // 2]], axis=0).T)
        cp, sn = _rope_tables(TP, sp)
        cs_, ss_ = _rope_tables(TS, ss)
        g = gbase.copy()
        g[:, GC_FL + 0] = 1.0 if half == 1 else 0.0
        g[:, GC_FL + 1] = 1.0 if half == 0 else 0.0
        g[:, GC_FL + 2] = 1.0 if half == 1 else 0.0
        g[:, GC_FL + 3] = 1.0 if half == 0 else 0.0
        m = dict(shared)
        m.update({
            "xT": xT, "pT": pT,
            "cosT": np.ascontiguousarray(np.concatenate([cp, cs_], axis=1)),
            "sinT": np.ascontiguousarray(np.concatenate([sn, ss_], axis=1)),
            "invcT": np.ascontiguousarray(np.concatenate([_invc(TP, sp, TP // 2), _invc(TS, ss, TS // 2)], axis=1)),
            "gpack": g,
        })
        in_maps.append(m)
    res = run_bass_kernel_spmd(nc, in_maps, core_ids=list(range(8)))
    if STAGE == 30:
        np.save("_dbgT.npy", np.asarray(res.results[0]["dbgT"]))
    y_prompt = np.empty((4, TP, D), np.float32)
    y_sample = np.empty((4, TS, D), np.float32)
    for c in range(8):
        i, half = c // 2, c % 2
        sp, ss = half * (TP // 2), half * (TS // 2)
        yT = np.asarray(res.results[c]["yT"])
        y_prompt[i, sp:sp + TP // 2] = yT[:, :TP // 2].T
        y_sample[i, ss:ss + TS // 2] = yT[:, TP // 2:].T
    return (y_prompt, y_sample)
```
